# Optimizing a Trainium2 kernel written in Bass

```python
import jax, jax.numpy as jnp
from jax import lax
import numpy as np

D_MODEL = 1024
BATCH = 8
SEQ = 4096
DEPTH = 4

CHUNK = 64
SUB = 16
N_SUB = CHUNK // SUB
EPS = 1e-6

A_HEADS = 8
A_DK = 128
A_KEY = A_HEADS * A_DK
A_DV = D_MODEL // A_HEADS
B_HEADS = 4
B_KEY = D_MODEL // 2
B_DK = B_KEY // B_HEADS
B_DV = D_MODEL // B_HEADS
B_RANK = 16
B_GATE_NORM = 16.0
C_HEADS = 8
C_DK = 128
C_KEY = C_HEADS * C_DK
C_DV = D_MODEL // C_HEADS
C_CONV = 4
C_QKV = 2 * C_KEY + D_MODEL
D_FF = 4 * D_MODEL
N_BRANCH = 3

SPLITS = (A_KEY, A_KEY, D_MODEL, D_MODEL,
          B_KEY, B_KEY, D_MODEL, B_RANK, D_MODEL,
          C_QKV, C_HEADS, C_HEADS, D_MODEL,
          N_BRANCH * D_MODEL)
N_IN = 2 * A_KEY + 2 * D_MODEL + 2 * B_KEY + 2 * D_MODEL + B_RANK + C_QKV + 2 * C_HEADS + D_MODEL + N_BRANCH * D_MODEL

kernel_name = 'hybrid_hgrn2_gla_gdn_encoder'


def rmsnorm(x, w):
    xf = x.astype(jnp.float32)
    y = xf * lax.rsqrt(jnp.mean(xf * xf, axis=-1, keepdims=True) + EPS)
    return (y * w.astype(jnp.float32)).astype(x.dtype)


def l2norm(x):
    return x * lax.rsqrt(jnp.sum(x * x, axis=-1, keepdims=True) + EPS)


def to_chunks(t):
    b, t_len, h, d = t.shape
    return t.reshape(b, t_len // CHUNK, CHUNK, h, d).transpose(1, 0, 3, 2, 4)


def from_chunks(t):
    nc, b, h, c, d = t.shape
    return t.transpose(1, 0, 3, 2, 4).reshape(b, nc * c, h, d)


def scalar_chunks(t):
    b, t_len, h = t.shape
    return t.reshape(b, t_len // CHUNK, CHUNK, h).transpose(1, 0, 3, 2)


def gated_linear_attention(q, k, v, log_f):
    bsz, _, h, kd = q.shape
    vd = v.shape[-1]
    qc, kc, vc, gc = (to_chunks(t.astype(jnp.float32)) for t in (q, k, v, log_f))
    tri = jnp.tril(jnp.ones((SUB, SUB), bool))
    strict_sub = jnp.tril(jnp.ones((N_SUB, N_SUB), bool), -1)
    eye_sub = jnp.eye(N_SUB, dtype=jnp.float32)

    def step(state, inp):
        qi, ki, vi, gi = inp
        b = jnp.cumsum(gi, axis=-2)
        o_inter = jnp.einsum('bhck,bhkv->bhcv', qi * jnp.exp(b), state)
        qs, ks, bs = (t.reshape(bsz, h, N_SUB, SUB, kd) for t in (qi, ki, b))
        b_excl = (b - gi).reshape(bsz, h, N_SUB, SUB, kd)
        diff = jnp.where(tri[:, :, None], bs[..., :, None, :] - bs[..., None, :, :], -jnp.inf)
        a_diag = jnp.einsum('bhnik,bhnijk,bhnjk->bhnij', qs, jnp.exp(diff), ks)
        b_start = b_excl[..., 0, :]
        b_end = bs[..., -1, :]
        q_off = qs * jnp.exp(bs - b_start[..., None, :])
        k_off = ks * jnp.exp(b_end[..., None, :] - bs)
        pair = jnp.where(strict_sub[:, :, None],
                         b_start[:, :, :, None, :] - b_end[:, :, None, :, :], -jnp.inf)
        a_off = jnp.einsum('bhsik,bhstk,bhtjk->bhsitj', q_off, jnp.exp(pair), k_off)
        a = (a_off + jnp.einsum('bhnij,nm->bhnimj', a_diag, eye_sub)).reshape(bsz, h, CHUNK, CHUNK)
        o = o_inter + jnp.einsum('bhij,bhjv->bhiv', a, vi)
        b_last = b[..., -1, :]
        state = jnp.exp(b_last)[..., None] * state + jnp.einsum(
            'bhck,bhcv->bhkv', ki * jnp.exp(b_last[..., None, :] - b), vi)
        return state, o

    s0 = jnp.zeros((bsz, h, kd, vd), jnp.float32)
    _, o = lax.scan(step, s0, (qc, kc, vc, gc))
    return from_chunks(o)


def gated_delta_rule(q, k, v, log_a, beta):
    bsz, _, h, kd = q.shape
    vd = v.shape[-1]
    qc, kc, vc = (to_chunks(t.astype(jnp.float32)) for t in (q, k, v))
    gc = scalar_chunks(log_a.astype(jnp.float32))
    bc = scalar_chunks(beta.astype(jnp.float32))
    b = jnp.cumsum(gc, axis=-1)
    causal = jnp.tril(jnp.ones((CHUNK, CHUNK), bool))
    strict = jnp.tril(jnp.ones((CHUNK, CHUNK), bool), -1)
    decay = jnp.exp(jnp.where(causal, b[..., :, None] - b[..., None, :], -jnp.inf))
    kk = jnp.einsum('nbhik,nbhjk->nbhij', kc, kc)
    m = jnp.eye(CHUNK, dtype=jnp.float32) + jnp.where(strict, bc[..., :, None] * kk * decay, 0.0)
    rhs = jnp.concatenate([vc * bc[..., None], kc * (bc * jnp.exp(b))[..., None]], axis=-1)
    sol = lax.linalg.triangular_solve(m, rhs, left_side=True, lower=True, unit_diagonal=True)
    u, w = sol[..., :vd], sol[..., vd:]
    qk = jnp.einsum('nbhik,nbhjk->nbhij', qc, kc) * decay
    q_dec = qc * jnp.exp(b)[..., None]
    k_dec = kc * jnp.exp(b[..., -1:] - b)[..., None]
    a_last = jnp.exp(b[..., -1])

    def step(state, inp):
        u_i, w_i, qk_i, qd_i, kd_i, al_i = inp
        v_new = u_i - jnp.einsum('bhck,bhkv->bhcv', w_i, state)
        o = jnp.einsum('bhck,bhkv->bhcv', qd_i, state) + jnp.einsum('bhij,bhjv->bhiv', qk_i, v_new)
        state = al_i[..., None, None] * state + jnp.einsum('bhck,bhcv->bhkv', kd_i, v_new)
        return state, o

    s0 = jnp.zeros((bsz, h, kd, vd), jnp.float32)
    _, o = lax.scan(step, s0, (u, w, qk, q_dec, k_dec, a_last))
    return from_chunks(o)


def causal_short_conv(x, w):
    kw = w.shape[0]
    t_len = x.shape[1]
    xp = jnp.pad(x, ((0, 0), (kw - 1, 0), (0, 0)))
    out = xp[:, 0:t_len] * w[0]
    for j in range(1, kw):
        out = out + xp[:, j:j + t_len] * w[j]
    return out


def hybrid_mixer(h, w_in, lb, gla_w_gk, gla_b_gk, gdn_conv, gdn_a_log, gdn_dt_bias,
                 hgrn_onorm, gla_onorm, gdn_onorm, w_out):
    bsz, t_len, _ = h.shape
    proj = h @ w_in
    cuts = [int(c) for c in np.cumsum(SPLITS)[:-1]]
    (a_q, a_f, a_i, a_g, b_q, b_k, b_v, b_gk, b_g,
     c_qkv, c_a, c_b, c_g, merge) = jnp.split(proj, cuts, axis=-1)

    z = a_f.astype(jnp.float32).reshape(bsz, t_len, A_HEADS, A_DK)
    lb_h = lb.reshape(A_HEADS, A_DK)
    log_f_a = jnp.logaddexp(jnp.log(lb_h), jnp.log1p(-lb_h) + jax.nn.log_sigmoid(z))
    k_a = (1.0 - lb_h) * jax.nn.sigmoid(-z)
    q_a = jax.nn.silu(a_q).reshape(bsz, t_len, A_HEADS, A_DK)
    o_a = gated_linear_attention(q_a, k_a, a_i.reshape(bsz, t_len, A_HEADS, A_DV), log_f_a)
    y_a = (rmsnorm(o_a, hgrn_onorm) * jax.nn.silu(a_g.reshape(bsz, t_len, A_HEADS, A_DV))).reshape(bsz, t_len, D_MODEL)

    q_b = b_q.reshape(bsz, t_len, B_HEADS, B_DK) * (B_DK ** -0.5)
    k_b = b_k.reshape(bsz, t_len, B_HEADS, B_DK)
    gk = (b_gk @ gla_w_gk + gla_b_gk).astype(jnp.float32)
    log_f_b = (jax.nn.log_sigmoid(gk) / B_GATE_NORM).reshape(bsz, t_len, B_HEADS, B_DK)
    o_b = gated_linear_attention(q_b, k_b, b_v.reshape(bsz, t_len, B_HEADS, B_DV), log_f_b)
    y_b = (rmsnorm(o_b, gla_onorm) * jax.nn.silu(b_g.reshape(bsz, t_len, B_HEADS, B_DV))).reshape(bsz, t_len, D_MODEL)

    qkv = jax.nn.silu(causal_short_conv(c_qkv, gdn_conv))
    q_c = l2norm(qkv[..., :C_KEY].astype(jnp.float32).reshape(bsz, t_len, C_HEADS, C_DK)) * (C_DK ** -0.5)
    k_c = l2norm(qkv[..., C_KEY:2 * C_KEY].astype(jnp.float32).reshape(bsz, t_len, C_HEADS, C_DK))
    v_c = qkv[..., 2 * C_KEY:].reshape(bsz, t_len, C_HEADS, C_DV)
    log_a_c = -jnp.exp(gdn_a_log.astype(jnp.float32)) * jax.nn.softplus((c_a + gdn_dt_bias).astype(jnp.float32))
    beta_c = jax.nn.sigmoid(c_b.astype(jnp.float32))
    o_c = gated_delta_rule(q_c, k_c, v_c, log_a_c, beta_c)
    y_c = (rmsnorm(o_c, gdn_onorm) * jax.nn.silu(c_g.reshape(bsz, t_len, C_HEADS, C_DV))).reshape(bsz, t_len, D_MODEL)

    g = jax.nn.sigmoid(merge.astype(jnp.float32)).reshape(bsz, t_len, N_BRANCH, D_MODEL)
    y = g[:, :, 0] * y_a + g[:, :, 1] * y_b + g[:, :, 2] * y_c
    return (y.astype(h.dtype) @ w_out).astype(h.dtype)


def squared_relu_mlp(h, w_up, w_down):
    return jnp.square(jax.nn.relu(h @ w_up)) @ w_down


def setup_inputs(seed: int = 0) -> dict:
    key = jax.random.key(seed)
    ks = jax.random.split(key, 20)
    f32 = jnp.float32

    def gain(k, n):
        return 1.0 + 0.02 * jax.random.normal(k, (DEPTH, n), f32)

    dt = jnp.exp(jax.random.uniform(ks[11], (DEPTH, C_HEADS), f32, np.log(1e-3), np.log(1e-1)))
    return {
        'x': jax.random.normal(ks[0], (BATCH, SEQ, D_MODEL), f32),
        'ln_mix_pre': gain(ks[1], D_MODEL),
        'ln_mix_post': gain(ks[2], D_MODEL),
        'ln_mlp_pre': gain(ks[3], D_MODEL),
        'ln_mlp_post': gain(ks[4], D_MODEL),
        'w_in': jax.random.normal(ks[5], (DEPTH, D_MODEL, N_IN), f32) * D_MODEL ** -0.5,
        'hgrn_lb_logits': 0.1 * jax.random.normal(ks[6], (DEPTH, A_KEY), f32),
        'gla_w_gk': jax.random.normal(ks[7], (DEPTH, B_RANK, B_KEY), f32) * B_RANK ** -0.5,
        'gla_b_gk': 0.1 * jax.random.normal(ks[8], (DEPTH, B_KEY), f32),
        'gdn_conv': jax.random.normal(ks[9], (DEPTH, C_CONV, C_QKV), f32) * C_CONV ** -0.5,
        'gdn_a_log': jnp.log(jax.random.uniform(ks[10], (DEPTH, C_HEADS), f32, 1.0, 16.0)),
        'gdn_dt_bias': dt + jnp.log(-jnp.expm1(-dt)),
        'hgrn_onorm': gain(ks[12], A_DV),
        'gla_onorm': gain(ks[13], B_DV),
        'gdn_onorm': gain(ks[14], C_DV),
        'w_out': jax.random.normal(ks[15], (DEPTH, D_MODEL, D_MODEL), f32) * D_MODEL ** -0.5,
        'w_up': jax.random.normal(ks[16], (DEPTH, D_MODEL, D_FF), f32) * D_MODEL ** -0.5,
        'w_down': jax.random.normal(ks[17], (DEPTH, D_FF, D_MODEL), f32) * D_FF ** -0.5,
    }


def reference(x, ln_mix_pre, ln_mix_post, ln_mlp_pre, ln_mlp_post, w_in, hgrn_lb_logits,
              gla_w_gk, gla_b_gk, gdn_conv, gdn_a_log, gdn_dt_bias, hgrn_onorm, gla_onorm,
              gdn_onorm, w_out, w_up, w_down):
    lb_cum = jnp.cumsum(jax.nn.softmax(hgrn_lb_logits.astype(jnp.float32), axis=0), axis=0)
    for layer in range(DEPTH):
        lb = jnp.clip(lb_cum[layer] - lb_cum[0], 0.0, 1.0)
        h = rmsnorm(x, ln_mix_pre[layer])
        mix = hybrid_mixer(h, w_in[layer], lb, gla_w_gk[layer], gla_b_gk[layer], gdn_conv[layer],
                           gdn_a_log[layer], gdn_dt_bias[layer], hgrn_onorm[layer], gla_onorm[layer],
                           gdn_onorm[layer], w_out[layer])
        x = x + rmsnorm(mix, ln_mix_post[layer]).astype(x.dtype)
        h = rmsnorm(x, ln_mlp_pre[layer])
        x = x + rmsnorm(squared_relu_mlp(h, w_up[layer], w_down[layer]), ln_mlp_post[layer]).astype(x.dtype)
    return x
```

```python
import numpy as np, time, contextlib
import concourse.bass as bass
import concourse.mybir as mybir
from concourse.bass_utils import run_bass_kernel_spmd
F32 = mybir.dt.float32; BF16 = mybir.dt.bfloat16
AF = mybir.ActivationFunctionType; ALU = mybir.AluOpType; AX = mybir.AxisListType
D = 1024; DFF = 4096; EPS = 1e-6

class Buf:
    __slots__ = ("name", "w", "r")
    def __init__(self, name=""):
        self.name = name; self.w = None; self.r = {}

class Prog:
    ENGS = ("pe", "act", "dve", "pool", "sp")
    def __init__(self, nc, es, n_dma_sems=32):
        self.nc = nc
        self.lists = {e: [] for e in self.ENGS}
        self.sem = {e: es.enter_context(nc.semaphore("s_" + e)) for e in self.ENGS}
        self.cnt = {e: 0 for e in self.ENGS}
        self.waited = {e: {} for e in self.ENGS}
        self.dsems = [es.enter_context(nc.semaphore(f"d{i}")) for i in range(n_dma_sems)]
        self.dcnt = [0] * n_dma_sems
        self.dnext = 0
        self.ninst = 0
    def _wait(self, eng, key, semh, val):
        if self.waited[eng].get(key, 0) >= val:
            return
        self.waited[eng][key] = val
        self.lists[eng].append(lambda e, semh=semh, val=val: e.wait_ge(semh, val))
    def _deps(self, eng, reads, writes):
        deps = {}
        def add(t):
            key, semh, val = t
            if key not in deps or deps[key][1] < val: deps[key] = (semh, val)
        for b in reads:
            if b.w is not None: add(b.w)
        for b in writes:
            if b.w is not None: add(b.w)
            for key, (semh, val) in b.r.items(): add((key, semh, val))
        for key, (semh, val) in deps.items():
            self._wait(eng, key, semh, val)
    def _mark(self, ticket, reads, writes):
        key, semh, val = ticket
        for b in reads:
            if key not in b.r or b.r[key][1] < val: b.r[key] = (semh, val)
        for b in writes:
            b.w = ticket; b.r = {}
    def op(self, eng, fn, reads=(), writes=(), inc=True):
        self._deps(eng, reads, writes)
        self.ninst += 1
        if inc:
            self.cnt[eng] += 1
            n = self.cnt[eng]; semh = self.sem[eng]
            self.lists[eng].append(lambda e, fn=fn, semh=semh: fn(e).then_inc(semh, 1))
            self._mark((eng, semh, n), reads, writes)
        else:
            self.lists[eng].append(lambda e, fn=fn: fn(e))
    def dma(self, eng, out, in_, reads=(), writes=(), slow=False):
        self._deps(eng, reads, writes)
        i = self.dnext; self.dnext = (self.dnext + 1) % len(self.dsems)
        semh = self.dsems[i]
        if self.dcnt[i] > 0:
            self._wait(eng, ("d", i), semh, 16 * self.dcnt[i])
        self.dcnt[i] += 1
        val = 16 * self.dcnt[i]
        self.lists[eng].append(lambda e, out=out, in_=in_, semh=semh, slow=slow: e.dma_start(out=out, in_=in_, allow_slow_non_contiguous=slow).then_inc(semh, 16))
        self._mark((("d", i), semh, val), reads, writes)
        self.ninst += 1
    def barrier(self):
        for eng in self.ENGS:
            for e2 in self.ENGS:
                if e2 != eng and self.cnt[e2] > 0:
                    self._wait(eng, e2, self.sem[e2], self.cnt[e2])
            for i, semh in enumerate(self.dsems):
                if self.dcnt[i] > 0:
                    self._wait(eng, ("d", i), semh, 16 * self.dcnt[i])
    def finish(self, eng, bufs):
        self._deps(eng, bufs, ())
    def emit(self):
        nc = self.nc
        with nc.Block() as block:
            @block.tensor
            def _(e):
                for f in self.lists["pe"]: f(e)
            @block.scalar
            def _(e):
                for f in self.lists["act"]: f(e)
            @block.vector
            def _(e):
                for f in self.lists["dve"]: f(e)
            @block.gpsimd
            def _(e):
                for f in self.lists["pool"]: f(e)
            @block.sync
            def _(e):
                for f in self.lists["sp"]: f(e)


NIN = 14368
MG = 11296
def a_cols(h): return [(h*128,128),(1024+h*128,128),(2048+h*128,128),(3072+h*128,128),(MG+h*128,128)]
def b_cols(h): return [(4096+h*128,128),(4608+h*128,128),(5120+h*256,256),(6160+h*256,256),(MG+1024+h*256,256)]
def c_cols(h): return [(7184+h*128,128),(8208+h*128,128),(9232+h*128,128),(10272+h*128,128),(MG+2048+h*128,128)]
MISC = [(6144,16),(10256,16)]
GROUPS = [a_cols(h) for h in range(8)] + [b_cols(h) for h in range(4)] + [c_cols(h) for h in range(8)] + [MISC]
GN = [sum(n for _, n in g) for g in GROUPS]

def make_consts():
    c = np.zeros((128, 2048), np.float32)
    i = np.arange(128)
    same = (i[:, None] // 64) == (i[None, :] // 64)
    c[:, 0:128] = np.eye(128)
    incl = same & (i[None, :] >= i[:, None])
    strT = same & (i[None, :] > i[:, None])
    strn = same & (i[None, :] < i[:, None])
    c[:, 128:256] = incl; c[:, 256:384] = strT; c[:, 384:512] = strn
    c[:, 512:640] = (incl - 1.0) * 30000.0; c[:, 640:768] = (strT - 1.0) * 30000.0; c[:, 768:896] = (strn - 1.0) * 30000.0
    c[:, 896:1024] = 1.0
    m = np.ones(512, np.float32); m[::64] = 0.0
    c[:, 1024:1536] = m[None, :]
    return c

class K:
    def __init__(self, T, DEPTH, do_mix=True, do_mlp=True):
        self.T, self.DEPTH = T, DEPTH
        self.NB = T // 512
        self.do_mix, self.do_mlp = do_mix, do_mlp
        nc = self.nc = bass.Bass("TRN2", target_bir_lowering=False)
        self.es = contextlib.ExitStack()
        dt = lambda n, s, d=F32, k="ExternalInput": nc.dram_tensor(n, s, d, kind=k).ap()
        L = DEPTH
        self.x = dt("x", [T, D]); self.out = dt("out", [T, D], F32, "ExternalOutput")
        self.ln = {n: dt(n, [L, D]) for n in ("ln_mix_pre", "ln_mix_post", "ln_mlp_pre", "ln_mlp_post")}
        self.w_in = dt("w_in", [L, D, NIN]); self.lbl = dt("hgrn_lb_logits", [L, 1024])
        self.wgk = dt("gla_w_gk", [L, 16, 512]); self.bgk = dt("gla_b_gk", [L, 512])
        self.conv = dt("gdn_conv", [L, 4, 3072]); self.alog = dt("gdn_a_log", [L, 8]); self.dtb = dt("gdn_dt_bias", [L, 8])
        self.on_a = dt("hgrn_onorm", [L, 128]); self.on_b = dt("gla_onorm", [L, 256]); self.on_c = dt("gdn_onorm", [L, 128])
        self.w_out = dt("w_out", [L, D, D]); self.w_up = dt("w_up", [L, D, DFF]); self.w_down = dt("w_down", [L, DFF, D])
        self.consts = dt("consts", [128, 2048])
        self.win_img = dt("win_img", [L, len(GROUPS), 128, 8192], BF16, "Internal")
        self.wup_img = dt("wup_img", [L, 8, 128, 4096], BF16, "Internal")
        self.wdn_img = dt("wdn_img", [L, 4, 128, 8192], BF16, "Internal")
        self.wout_img = dt("wout_img", [L, 128, 8192], BF16, "Internal")

    def sb(self, name, shape, dtype=F32):
        return self.es.enter_context(self.nc.sbuf_tensor(name, shape, dtype))

    debug = False
    def dbg(self, name, ap, reads):
        if not self.debug: return
        if not hasattr(self, "dbgb"): self.dbgb = Buf("dbg"); self.dbgn = set()
        if name in self.dbgn: return
        self.dbgn.add(name)
        d = self.nc.dram_tensor("dbg_" + name, list(ap.shape), ap.dtype, kind="ExternalOutput").ap()
        self.P.dma("sp", d, ap, reads=reads, writes=[self.dbgb])

    ARENA = 61440
    def ar(self, shape, dtype=F32):
        if not hasattr(self, "arena"):
            self.arena = self.sb("arena", [128, self.ARENA], BF16); self.aoff = 0
        n = int(np.prod(shape[1:])) * (2 if dtype == F32 else 1)
        a = self.arena[0:shape[0], self.aoff:self.aoff + n]
        self.aoff += n
        assert self.aoff <= self.ARENA, ("arena overflow", self.aoff)
        if dtype == F32: a = a.bitcast(F32)
        if len(shape) == 3: a = a.rearrange("p (a b) -> p a b", a=shape[1])
        return a

    def build(self):
        nc = self.nc
        with self.es:
            P = self.P = Prog(nc, self.es)
            self.psum = [self.es.enter_context(nc.psum_tensor(f"ps{i}", [128, 512], F32)) for i in range(8)]
            self.pb = [Buf(f"ps{i}") for i in range(8)]
            self.ps_rr = 0
            self.setup_consts()
            self.convert_all()
            P.barrier()
            self.main()
            P.emit()
        return nc

    def setup_consts(self):
        P = self.P
        self.cst = self.sb("cst", [128, 2048]); self.cstb = Buf("cst")
        P.dma("sp", self.cst[:], self.consts[:, :], writes=[self.cstb])
        self.identb = self.sb("identb", [128, 128], BF16); self.onesb = self.sb("onesb", [128, 128], BF16)
        self.inclb = self.sb("inclb", [128, 128], BF16)
        P.op("dve", lambda e: e.tensor_copy(out=self.identb[:], in_=self.cst[:, 0:128]), reads=[self.cstb], writes=[self.cstb])
        P.op("dve", lambda e: e.tensor_copy(out=self.onesb[:], in_=self.cst[:, 896:1024]), reads=[self.cstb], writes=[self.cstb])
        P.op("dve", lambda e: e.tensor_copy(out=self.inclb[:], in_=self.cst[:, 128:256]), reads=[self.cstb], writes=[self.cstb])
        self.ident = self.cst[:, 0:128]; self.ones = self.cst[:, 896:1024]
        L = self.DEPTH
        P_ = self.P
        rows = []
        def addrows(name, ap2d):
            r0 = sum(r.shape[0] for _, r in rows) if rows else 0
            rows.append((name, ap2d)); return r0
        self.prow = {}
        self.prow["ln_mix_pre"] = addrows("a", self.ln["ln_mix_pre"].rearrange("l (c p) -> (l c) p", p=128))
        self.prow["ln_mlp_pre"] = addrows("b", self.ln["ln_mlp_pre"].rearrange("l (c p) -> (l c) p", p=128))
        self.prow["lbl"] = addrows("c", self.lbl.rearrange("l (c p) -> (l c) p", p=128))
        self.prow["bgk"] = addrows("d", self.bgk.rearrange("l (c p) -> (l c) p", p=128))
        self.prow["on_a"] = addrows("e", self.on_a)
        self.prow["on_b"] = addrows("f", self.on_b.rearrange("l (c p) -> (l c) p", p=128))
        self.prow["on_c"] = addrows("g", self.on_c)
        nr = sum(r.shape[0] for _, r in rows); assert nr <= 128
        self.prm = self.sb("prm", [128, 128]); self.smallb = Buf("small")
        self.PT = self.sb("PT", [128, 128 + 96 * L])
        P_.op("dve", lambda e: e.memset(self.prm[:], 0.0), writes=[self.smallb])
        r0 = 0
        for _, r in rows:
            P_.dma("sp", self.prm[r0:r0 + r.shape[0], :], r, writes=[self.smallb]); r0 += r.shape[0]
        pt, ptb = self.ps()
        P_.op("pe", lambda e: e.transpose(out=pt[:, 0:128], in_=self.prm[:], identity=self.ident), reads=[self.smallb, self.cstb], writes=[ptb])
        P_.op("dve", lambda e: e.tensor_copy(out=self.PT[:, 0:128], in_=pt[:, 0:128]), reads=[ptb], writes=[self.smallb])
        cv = self.conv.rearrange("l j (c p) -> (l j c) p", p=128)
        self.prm2 = self.sb("prm2", [128, 128])
        for i in range(0, 96 * L, 96):
            P_.dma("sp", self.prm2[0:96, :], cv[i:i + 96, :], reads=[self.smallb], writes=[self.smallb])
            pt, ptb = self.ps()
            P_.op("pe", lambda e, pt=pt: e.transpose(out=pt[:, 0:96], in_=self.prm2[0:96, :], identity=self.ident[0:96, 0:96]), reads=[self.smallb, self.cstb], writes=[ptb])
            P_.op("dve", lambda e, pt=pt, i=i: e.tensor_copy(out=self.PT[:, 128 + i:128 + i + 96], in_=pt[:, 0:96]), reads=[ptb], writes=[self.smallb])
        self.lnT = {n: self.PT[:, self.prow[n]:self.prow[n] + 8 * L].rearrange("p (l c) -> p l c", c=8) for n in ("ln_mix_pre", "ln_mlp_pre")}

    def convert_group(self, src, ranges, dst, stage, stb, img, imb, rr):
        P = self.P
        ncols = sum(n for _, n in ranges)
        iv = img[:, 0:8 * ncols].rearrange("p (k n) -> p k n", k=8)
        off = 0
        for (c0, n) in ranges:
            for s0 in range(0, n, 256):
                sn = min(256, n - s0)
                k = rr[0] % len(stage); rr[0] += 1
                sv = stage[k][:, 0:8 * sn].rearrange("p (k n) -> p k n", k=8)
                P.dma("sp", sv, src[:, c0 + s0:c0 + s0 + sn].rearrange("(k p) n -> p k n", p=128), writes=[stb[k]])
                eng = ("dve", "pool", "act")[rr[0] % 3]
                o = iv[:, :, off + s0:off + s0 + sn]
                if eng == "act":
                    P.op("act", lambda e, o=o, sv=sv: e.copy(out=o, in_=sv), reads=[stb[k]], writes=[imb])
                else:
                    P.op(eng, lambda e, o=o, sv=sv: e.tensor_copy(out=o, in_=sv), reads=[stb[k]], writes=[imb])
            off += n
        P.dma("sp", dst[:, 0:8 * ncols], img[:, 0:8 * ncols], reads=[imb], writes=[self.imgdram])

    def convert_all(self):
        P = self.P
        self.imgdram = Buf("imgdram")
        with contextlib.ExitStack() as es2:
            stage = [es2.enter_context(self.nc.sbuf_tensor(f"stg{i}", [128, 2048], F32)) for i in range(4)]
            stb = [Buf() for _ in stage]
            imgs = [es2.enter_context(self.nc.sbuf_tensor(f"img{i}", [128, 8192], BF16)) for i in range(2)]
            imbs = [Buf() for _ in imgs]
            rr = [0]; gi = 0
            for l in range(self.DEPTH):
                if self.do_mix:
                    for g, ranges in enumerate(GROUPS):
                        self.convert_group(self.w_in[l], ranges, self.win_img[l, g], stage, stb, imgs[gi % 2], imbs[gi % 2], rr); gi += 1
                    self.convert_group(self.w_out[l], [(0, 1024)], self.wout_img[l], stage, stb, imgs[gi % 2], imbs[gi % 2], rr); gi += 1
                if self.do_mlp:
                    for g in range(8):
                        self.convert_group(self.w_up[l], [(g * 512, 512)], self.wup_img[l, g], stage, stb, imgs[gi % 2], imbs[gi % 2], rr); gi += 1
                    for g in range(4):
                        self.convert_group(self.w_down[l][g * 1024:(g + 1) * 1024, :], [(0, 1024)], self.wdn_img[l, g], stage, stb, imgs[gi % 2], imbs[gi % 2], rr); gi += 1
            P.barrier()

    def ps(self):
        i = self.ps_rr; self.ps_rr = (i + 1) % 6
        return self.psum[i], self.pb[i]

    def norm_T(self, xres, xb, lnT_l, hT, hb, tmp):
        P = self.P
        junk, jb, ss, ssb, xn, xnb = tmp
        for t in range(4):
            P.op("act", lambda e, t=t: e.activation(out=junk[:], in_=xres[:, t, :], func=AF.Square, accum_out=ss[:, t:t + 1]), reads=[xb], writes=[jb, ssb])
            P.op("dve", lambda e, t=t: e.tensor_scalar(out=ss[:, 4 + t:5 + t], in0=ss[:, t:t + 1], scalar1=1.0 / D, scalar2=EPS, op0=ALU.mult, op1=ALU.add), reads=[ssb], writes=[ssb])
            P.op("act", lambda e, t=t: e.activation(out=ss[:, 20 + t:21 + t], in_=ss[:, 4 + t:5 + t], func=AF.Sqrt), reads=[ssb], writes=[ssb])
            P.op("dve", lambda e, t=t: e.reciprocal(out=ss[:, 8 + t:9 + t], in_=ss[:, 20 + t:21 + t]), reads=[ssb], writes=[ssb])
            P.op("act", lambda e, t=t: e.activation(out=xn[:], in_=xres[:, t, :], func=AF.Copy, scale=ss[:, 8 + t:9 + t]), reads=[xb, ssb], writes=[xnb])
            pt, ptb = self.ps()
            pv = pt[:].bitcast(BF16).rearrange("p (c n) -> p c n", c=8)
            for c in range(8):
                P.op("pe", lambda e, c=c, pv=pv: e.transpose(out=pv[:, c, :], in_=xn[:, c * 128:(c + 1) * 128], identity=self.identb[:]),
                     reads=[xnb, self.cstb], writes=[ptb], inc=(c == 7))
            P.op("dve", lambda e, t=t, pv=pv: e.tensor_tensor(out=hT[:, :, t * 128:(t + 1) * 128], in0=pv, in1=lnT_l.unsqueeze(2).broadcast_to([128, 8, 128]), op=ALU.mult),
                 reads=[ptb, self.smallb], writes=[hb])

    def post_norm_add(self, pA, pAb, pB, pBb, xres, xb, t, lnw, lnwb, tmp):
        P = self.P
        junk, jb, ss, ssb, xn, xnb = tmp
        P.op("act", lambda e: e.activation(out=junk[:, 0:512], in_=pA[:], func=AF.Square, accum_out=ss[:, 12:13]), reads=[pAb], writes=[jb, ssb])
        P.op("act", lambda e: e.activation(out=junk[:, 512:1024], in_=pB[:], func=AF.Square, accum_out=ss[:, 13:14]), reads=[pBb], writes=[jb, ssb])
        P.op("dve", lambda e: e.tensor_tensor(out=ss[:, 14:15], in0=ss[:, 12:13], in1=ss[:, 13:14], op=ALU.add), reads=[ssb], writes=[ssb])
        P.op("dve", lambda e: e.tensor_scalar(out=ss[:, 15:16], in0=ss[:, 14:15], scalar1=1.0 / D, scalar2=EPS, op0=ALU.mult, op1=ALU.add), reads=[ssb], writes=[ssb])
        P.op("act", lambda e: e.activation(out=ss[:, 17:18], in_=ss[:, 15:16], func=AF.Sqrt), reads=[ssb], writes=[ssb])
        P.op("dve", lambda e: e.reciprocal(out=ss[:, 16:17], in_=ss[:, 17:18]), reads=[ssb], writes=[ssb])
        for half, (pp, ppb) in enumerate(((pA, pAb), (pB, pBb))):
            sl = slice(half * 512, (half + 1) * 512)
            P.op("dve", lambda e, pp=pp, sl=sl: e.scalar_tensor_tensor(out=junk[:, sl], in0=pp[:], scalar=ss[:, 16:17], in1=lnw[:, sl], op0=ALU.mult, op1=ALU.mult),
                 reads=[ppb, ssb, lnwb], writes=[jb])
            P.op("pool", lambda e, sl=sl: e.tensor_tensor(out=xres[:, t, sl], in0=xres[:, t, sl], in1=junk[:, sl], op=ALU.add), reads=[jb, xb], writes=[xb])


    def t0_init(self):
        R = lambda n: self.ar([1, n])
        self.t0p = R(NIN); self.hrow = R(1024); self.mrow = R(1024); self.qrow = R(4096); self.st = R(128)
        self.t0w = [self.ar([128, 8, 256]) for _ in range(2)]; self.t0wb = [Buf(), Buf()]
        self.t0col = self.ar([128, 32]); self.colb = Buf("col")
        self.rowb = Buf("rows"); self.t0rr = 0
        self.x0row = self.sb("x0row", [1, 1024]); self.x0b = Buf("x0")

    def t0_tocol(self, row, n):
        P = self.P
        pt, ptb = self.ps()
        for c in range(n):
            P.op("pe", lambda e, c=c, pt=pt: e.matmul(pt[:, c:c + 1], lhsT=row[0:1, c * 128:(c + 1) * 128], rhs=self.cst[0:1, 896:897], start=True, stop=True),
                 reads=[self.rowb, self.x0b, self.cstb], writes=[ptb], inc=(c == n - 1))
        P.op("dve", lambda e, pt=pt: e.tensor_copy(out=self.t0col[:, 0:n], in_=pt[:, 0:n]), reads=[ptb], writes=[self.colb])

    def t0_rowmm(self, nk, W, N, out_row):
        P = self.P
        CB = 256
        for c0 in range(0, N, CB):
            n = min(CB, N - c0)
            pt, ptb = self.ps()
            for kg in range(nk // 8):
                i = self.t0rr % 2; self.t0rr += 1
                buf, bb = self.t0w[i], self.t0wb[i]
                P.dma("sp", buf[:, :, 0:n], W[kg * 1024:(kg + 1) * 1024, c0:c0 + n].rearrange("(k p) n -> p k n", p=128), writes=[bb])
                for k in range(8):
                    kk = kg * 8 + k
                    P.op("pe", lambda e, kk=kk, k=k, buf=buf, pt=pt, n=n: e.matmul(pt[0:1, 0:n], lhsT=self.t0col[:, kk:kk + 1], rhs=buf[:, k, 0:n], start=(kk == 0), stop=(kk == nk - 1)),
                         reads=[bb, self.colb], writes=[ptb], inc=(k == 7))
            P.op("act", lambda e, pt=pt, c0=c0, n=n: e.copy(out=out_row[:, c0:c0 + n], in_=pt[0:1, 0:n]), reads=[ptb], writes=[self.rowb])

    def t0_rstd(self, ss_ap, out_ap, scale):
        P = self.P; rb = [self.rowb]
        P.op("dve", lambda e: e.tensor_scalar(out=out_ap, in0=ss_ap, scalar1=scale, scalar2=EPS, op0=ALU.mult, op1=ALU.add), reads=rb, writes=rb)
        P.op("act", lambda e: e.activation(out=out_ap, in_=out_ap, func=AF.Sqrt), reads=rb, writes=rb)
        P.op("dve", lambda e: e.reciprocal(out=out_ap, in_=out_ap), reads=rb, writes=rb)

    def t0_rn(self, src, lnw_dram_row, dst):
        P = self.P; rb = [self.rowb]; st = self.st
        P.dma("sp", self.qrow[:, 0:1024], lnw_dram_row, reads=rb, writes=rb)
        P.op("act", lambda e: e.activation(out=self.qrow[:, 1024:2048], in_=src, func=AF.Square, accum_out=st[:, 0:1]), reads=rb + [self.x0b], writes=rb)
        self.t0_rstd(st[:, 0:1], st[:, 1:2], 1.0 / D)
        P.op("dve", lambda e: e.scalar_tensor_tensor(out=dst, in0=src, scalar=st[:, 1:2], in1=self.qrow[:, 0:1024], op0=ALU.mult, op1=ALU.mult), reads=rb + [self.x0b], writes=rb)

    def t0_headnorm_gate(self, o0, nh, dv, onorm_dram_row, g0, tmp0):
        P = self.P; rb = [self.rowb]; st = self.st; pr = self.t0p
        n = nh * dv
        o3 = pr[:, o0:o0 + n].rearrange("p (h d) -> p h d", h=nh); t3 = pr[:, tmp0:tmp0 + n].rearrange("p (h d) -> p h d", h=nh)
        P.op("dve", lambda e: e.tensor_tensor(out=pr[:, tmp0:tmp0 + n], in0=pr[:, o0:o0 + n], in1=pr[:, o0:o0 + n], op=ALU.mult), reads=rb, writes=rb)
        P.op("dve", lambda e: e.tensor_reduce(out=st[:, 32:32 + nh], in_=t3, axis=AX.X, op=ALU.add), reads=rb, writes=rb)
        self.t0_rstd(st[:, 32:32 + nh], st[:, 40:40 + nh], 1.0 / dv)
        P.op("dve", lambda e: e.tensor_tensor(out=o3, in0=o3, in1=st[:, 40:40 + nh].unsqueeze(2).broadcast_to([1, nh, dv]), op=ALU.mult), reads=rb, writes=rb)
        P.dma("sp", self.qrow[:, 0:dv], onorm_dram_row, reads=rb, writes=rb)
        P.op("dve", lambda e: e.tensor_tensor(out=o3, in0=o3, in1=self.qrow[:, 0:dv].unsqueeze(1).broadcast_to([1, nh, dv]), op=ALU.mult), reads=rb, writes=rb)
        P.op("act", lambda e: e.activation(out=pr[:, g0:g0 + n], in_=pr[:, g0:g0 + n], func=AF.Silu), reads=rb, writes=rb)
        P.op("dve", lambda e: e.tensor_tensor(out=pr[:, o0:o0 + n], in0=pr[:, o0:o0 + n], in1=pr[:, g0:g0 + n], op=ALU.mult), reads=rb, writes=rb)

    def t0_layer(self, l):
        P = self.P; rb = [self.rowb]; st = self.st; pr = self.t0p; L = self.DEPTH
        v = lambda a, b_: pr[:, a:b_]
        def dve(fn, extra=()): P.op("dve", fn, reads=rb + list(extra), writes=rb)
        def act(o, i, f, **kw): P.op("act", lambda e: e.activation(out=o, in_=i, func=f, **kw), reads=rb, writes=rb)
        def mul(o, a, b_): dve(lambda e: e.tensor_tensor(out=o, in0=a, in1=b_, op=ALU.mult))
        def red(o, i3): dve(lambda e: e.tensor_reduce(out=o, in_=i3, axis=AX.X, op=ALU.add))
        if l == 0:
            P.dma("sp", self.x0row[:], self.x[0:1, :], writes=[self.x0b])
        self.t0_rn(self.x0row[:], self.ln["ln_mix_pre"][l:l + 1, :], self.hrow)
        self.t0_tocol(self.hrow, 8)
        self.t0_rowmm(8, self.w_in[l], NIN, pr)
        P.dma("sp", self.qrow[:, 0:L * 1024], self.lbl.rearrange("l n -> (l n)").unsqueeze(0), reads=rb, writes=rb)
        act(self.qrow[:, 0:L * 1024], self.qrow[:, 0:L * 1024], AF.Exp)
        dve(lambda e: e.tensor_copy(out=self.hrow, in_=self.qrow[:, 0:1024]))
        for j in range(1, L):
            dve(lambda e, j=j: e.tensor_tensor(out=self.hrow, in0=self.hrow, in1=self.qrow[:, j * 1024:(j + 1) * 1024], op=ALU.add))
        dve(lambda e: e.memset(self.mrow, 0.0))
        for j in range(1, l + 1):
            dve(lambda e, j=j: e.tensor_tensor(out=self.mrow, in0=self.mrow, in1=self.qrow[:, j * 1024:(j + 1) * 1024], op=ALU.add))
        dve(lambda e: e.reciprocal(out=self.hrow, in_=self.hrow))
        mul(self.mrow, self.mrow, self.hrow)
        dve(lambda e: e.tensor_scalar(out=self.mrow, in0=self.mrow, scalar1=0.0, scalar2=1.0, op0=ALU.max, op1=ALU.min))
        dve(lambda e: e.tensor_scalar(out=self.hrow, in0=self.mrow, scalar1=-1.0, scalar2=1.0, op0=ALU.mult, op1=ALU.add))
        act(v(0, 1024), v(0, 1024), AF.Silu)
        act(v(1024, 2048), v(1024, 2048), AF.Sigmoid)
        mul(v(1024, 2048), v(1024, 2048), self.hrow)
        dve(lambda e: e.tensor_tensor(out=v(1024, 2048), in0=v(1024, 2048), in1=self.mrow, op=ALU.add))
        dve(lambda e: e.tensor_scalar(out=v(1024, 2048), in0=v(1024, 2048), scalar1=-1.0, scalar2=1.0, op0=ALU.mult, op1=ALU.add))
        mul(v(0, 1024), v(0, 1024), v(1024, 2048))
        red(st[:, 8:16], v(0, 1024).rearrange("p (h d) -> p h d", h=8))
        i3 = v(2048, 3072).rearrange("p (h d) -> p h d", h=8)
        mul(i3, i3, st[:, 8:16].unsqueeze(2).broadcast_to([1, 8, 128]))
        self.t0_headnorm_gate(2048, 8, 128, self.on_a[l:l + 1, :], 3072, 0)
        mul(v(4096, 4608), v(4096, 4608), v(4608, 5120))
        red(st[:, 16:20], v(4096, 4608).rearrange("p (h d) -> p h d", h=4))
        dve(lambda e: e.tensor_scalar(out=st[:, 16:20], in0=st[:, 16:20], scalar1=128.0 ** -0.5, scalar2=None, op0=ALU.mult))
        b3 = v(5120, 6144).rearrange("p (h d) -> p h d", h=4)
        mul(b3, b3, st[:, 16:20].unsqueeze(2).broadcast_to([1, 4, 256]))
        self.t0_headnorm_gate(5120, 4, 256, self.on_b[l:l + 1, :], 6160, 0)
        P.dma("sp", self.qrow[:, 0:3072], self.conv[l, 3:4, :], reads=rb, writes=rb)
        mul(v(7184, 10256), v(7184, 10256), self.qrow[:, 0:3072])
        act(v(7184, 10256), v(7184, 10256), AF.Silu)
        q3 = v(7184, 8208).rearrange("p (h d) -> p h d", h=8); k3 = v(8208, 9232).rearrange("p (h d) -> p h d", h=8)
        t3 = v(0, 1024).rearrange("p (h d) -> p h d", h=8)
        mul(v(0, 1024), v(7184, 8208), v(7184, 8208)); red(st[:, 48:56], t3)
        mul(v(0, 1024), v(8208, 9232), v(8208, 9232)); red(st[:, 56:64], t3)
        mul(v(0, 1024), v(7184, 8208), v(8208, 9232)); red(st[:, 64:72], t3)
        for a in (48, 56):
            dve(lambda e, a=a: e.tensor_scalar(out=st[:, a:a + 8], in0=st[:, a:a + 8], scalar1=EPS, scalar2=None, op0=ALU.add))
            act(st[:, a:a + 8], st[:, a:a + 8], AF.Sqrt)
            dve(lambda e, a=a: e.reciprocal(out=st[:, a:a + 8], in_=st[:, a:a + 8]))
        act(st[:, 72:80], v(10264, 10272), AF.Sigmoid)
        mul(st[:, 64:72], st[:, 64:72], st[:, 48:56]); mul(st[:, 64:72], st[:, 64:72], st[:, 56:64]); mul(st[:, 64:72], st[:, 64:72], st[:, 72:80])
        dve(lambda e: e.tensor_scalar(out=st[:, 64:72], in0=st[:, 64:72], scalar1=128.0 ** -0.5, scalar2=None, op0=ALU.mult))
        c3 = v(9232, 10256).rearrange("p (h d) -> p h d", h=8)
        mul(c3, c3, st[:, 64:72].unsqueeze(2).broadcast_to([1, 8, 128]))
        self.t0_headnorm_gate(9232, 8, 128, self.on_c[l:l + 1, :], 10272, 0)
        act(v(MG, NIN), v(MG, NIN), AF.Sigmoid)
        mul(self.hrow, v(MG, MG + 1024), v(2048, 3072))
        mul(v(MG + 1024, MG + 2048), v(MG + 1024, MG + 2048), v(5120, 6144))
        dve(lambda e: e.tensor_tensor(out=self.hrow, in0=self.hrow, in1=v(MG + 1024, MG + 2048), op=ALU.add))
        mul(v(MG + 2048, NIN), v(MG + 2048, NIN), v(9232, 10256))
        dve(lambda e: e.tensor_tensor(out=self.hrow, in0=self.hrow, in1=v(MG + 2048, NIN), op=ALU.add))
        if self.debug: self.dbg(f"t0y{l}", self.hrow, rb)
        self.t0_tocol(self.hrow, 8)
        self.t0_rowmm(8, self.w_out[l], 1024, self.mrow)
        self.t0_rn(self.mrow, self.ln["ln_mix_post"][l:l + 1, :], self.mrow)
        P.op("dve", lambda e: e.tensor_tensor(out=self.x0row[:], in0=self.x0row[:], in1=self.mrow, op=ALU.add), reads=rb + [self.x0b], writes=[self.x0b])
        self.t0_rn(self.x0row[:], self.ln["ln_mlp_pre"][l:l + 1, :], self.hrow)
        self.t0_tocol(self.hrow, 8)
        self.t0_rowmm(8, self.w_up[l], DFF, pr)
        act(v(0, DFF), v(0, DFF), AF.Relu)
        mul(v(0, DFF), v(0, DFF), v(0, DFF))
        self.t0_tocol(pr, 32)
        self.t0_rowmm(32, self.w_down[l], 1024, self.mrow)
        self.t0_rn(self.mrow, self.ln["ln_mlp_post"][l:l + 1, :], self.mrow)
        P.op("dve", lambda e: e.tensor_tensor(out=self.x0row[:], in0=self.x0row[:], in1=self.mrow, op=ALU.add), reads=rb + [self.x0b], writes=[self.x0b])
        if self.debug: self.dbg(f"t0x{l}", self.x0row[:], [self.x0b])

    def main(self):
        P = self.P; nc = self.nc
        T, L = self.T, self.DEPTH
        xres = self.sb("xres", [128, 4, D]); xb = Buf("xres")
        lnw = self.sb("lnw", [128, 2, D]); lnwb = Buf("lnw")
        junk = self.sb("junk", [128, D]); ss = self.sb("ss", [128, 32]); xn = self.sb("xn", [128, D], BF16)
        tmp = (junk, Buf(), ss, Buf(), xn, Buf())
        hT = self.sb("hT", [128, 8, 512], BF16); hb = Buf("hT")
        if self.do_mix:
            self.mix_init()
            if getattr(self, 'do_c', False): self.c_init()
        self.do_t0 = getattr(self, 'do_t0', True) and self.do_mix and self.do_mlp
        if self.do_t0:
            self.aoff = 0; self.t0_init()
        self.aoff = 0
        wup = [self.ar([128, 8, 512], BF16) for i in range(2)]; wupb = [Buf(), Buf()]
        wdn = self.ar([128, 32, D], BF16); wdnb = Buf()
        uT = self.ar([128, 32, 512], BF16); ub = [Buf() for _ in range(32)]
        rl = [self.ar([128, 512]) for i in range(2)]; rlb = [Buf(), Buf()]
        outb = Buf("out")
        for l in range(L):
            if self.do_t0:
                P.barrier(); self.t0_layer(l); P.barrier()
            P.dma("sp", lnw[:, 0, :], self.ln["ln_mix_post"][l].partition_broadcast(128), writes=[lnwb], slow=True)
            P.dma("sp", lnw[:, 1, :], self.ln["ln_mlp_post"][l].partition_broadcast(128), writes=[lnwb], slow=True)
            for b in range(self.NB):
                src = self.x if l == 0 else self.out
                P.dma("sp", xres[:], src[b * 512:(b + 1) * 512, :].rearrange("(t p) d -> p t d", p=128), reads=[outb], writes=[xb])
                if self.do_mix:
                    self.mixer_block(l, b, xres, xb, hT, hb, tmp, lnw, lnwb)
                    P.barrier()
                if self.do_mlp:
                    self.norm_T(xres, xb, self.lnT["ln_mlp_pre"][:, l, :], hT, hb, tmp)
                    for g in range(4):
                        P.dma("sp", wdn[:, g * 8:(g + 1) * 8, :], self.wdn_img[l, g].rearrange("p (k n) -> p k n", k=8), reads=[self.imgdram], writes=[wdnb])
                    for g in range(8):
                        w = wup[g % 2]; wb = wupb[g % 2]
                        P.dma("sp", w[:], self.wup_img[l, g].rearrange("p (k n) -> p k n", k=8), reads=[self.imgdram], writes=[wb])
                        for c in range(4):
                            fc = g * 4 + c
                            pt, ptb = self.ps()
                            for k in range(8):
                                P.op("pe", lambda e, k=k, c=c, w=w, pt=pt: e.matmul(pt[:], lhsT=w[:, k, c * 128:(c + 1) * 128], rhs=hT[:, k, :], start=(k == 0), stop=(k == 7)),
                                     reads=[wb, hb], writes=[ptb], inc=(k == 7))
                            r = rl[fc % 2]; rb = rlb[fc % 2]
                            P.op("act", lambda e, r=r, pt=pt: e.activation(out=r[:], in_=pt[:], func=AF.Relu), reads=[ptb], writes=[rb])
                            P.op("dve", lambda e, r=r, fc=fc: e.tensor_tensor(out=uT[:, fc, :], in0=r[:], in1=r[:], op=ALU.mult), reads=[rb], writes=[ub[fc]])
                    for t in range(4):
                        pA, pAb = self.ps(); pB, pBb = self.ps()
                        for fc in range(32):
                            for half, (pp, ppb) in enumerate(((pA, pAb), (pB, pBb))):
                                P.op("pe", lambda e, fc=fc, half=half, pp=pp, t=t: e.matmul(pp[:], lhsT=uT[:, fc, t * 128:(t + 1) * 128], rhs=wdn[:, fc, half * 512:(half + 1) * 512], start=(fc == 0), stop=(fc == 31)),
                                     reads=[ub[fc], wdnb], writes=[ppb], inc=(fc == 31))
                        self.post_norm_add(pA, pAb, pB, pBb, xres, xb, t, lnw[:, 1, :], lnwb, tmp)
                    P.barrier()
                P.dma("sp", self.out[b * 512:(b + 1) * 512, :].rearrange("(t p) d -> p t d", p=128), xres[:], reads=[xb], writes=[outb])
                if b == 0 and self.do_t0:
                    P.dma("sp", self.out[0:1, :], self.x0row[:], reads=[self.x0b, outb], writes=[outb])
        P.finish("sp", [outb] + ([self.dbgb] if hasattr(self, "dbgb") else []))
        P.barrier()

    def mixer_block(self, *a): pass
    def mix_init(self): pass

@contextlib.contextmanager
def nc_allow(nc):
    with nc.allow_non_contiguous_dma(reason="small param load"):
        yield

def host_inputs(inputs, T, DEPTH):
    B = inputs["x"].shape[0]
    maps = []
    cst = globals()['make_consts']()
    for c in range(B):
        m = {"x": np.ascontiguousarray(inputs["x"][c, :T]), "consts": cst}
        for k, v in inputs.items():
            if k != "x": m[k] = np.ascontiguousarray(v[:DEPTH])
        maps.append(m)
    return maps


class K3(K):
    def __init__(self, T, DEPTH, do_mix=True, do_mlp=True, do_c=True):
        super().__init__(T, DEPTH, do_mix, do_mlp)
        self.do_c = do_c
        self.minit = False

    def mix_init(self):
        P = self.P; L = self.DEPTH
        self.minit = True
        sb = self.sb
        ar = lambda name, shape, dtype=F32: self.ar(shape, dtype)
        self.aoff = 0
        self.wsb = [ar(f"wsb{i}", [128, 8192], BF16) for i in range(2)]; self.wsbb = [Buf(), Buf()]
        self.wloaded = {}
        self.tt = [ar(f"tt{i}", [128, 512]) for i in range(8)]; self.ttb = [Buf() for _ in range(8)]
        self.qt = ar("qt", [128, 512], BF16); self.kt = ar("kt", [128, 512], BF16); self.kh = ar("kh", [128, 512], BF16)
        self.qtb, self.ktb, self.khb = Buf(), Buf(), Buf()
        self.khT = ar("khT", [128, 4, 128], BF16); self.khTb = Buf()
        self.Vsb = ar("Vsb", [128, 4, 256], BF16); self.Vb = Buf()
        self.AT = ar("AT", [128, 4, 128], BF16); self.ATb = Buf()
        self.Sp = ar("Sp", [128, 256], BF16); self.Spb = Buf()
        self.e2 = sb("e2", [128, 8, 2]); self.e2b = Buf()
        self.SA = sb("SA", [128, 8, 128]); self.SB = sb("SB", [128, 4, 256]); self.SAb = [Buf() for _ in range(8)]; self.SBb = [Buf() for _ in range(4)]
        self.yacc = ar("yacc", [128, 8, 512]); self.yb = [Buf() for _ in range(8)]
        self.yT = ar("yT", [128, 8, 512], BF16); self.yTb = Buf()
        self.osq = [ar(f"osq{i}", [128, 512], BF16) for i in range(2)]; self.osqb = [Buf(), Buf()]
        self.ob = [ar(f"ob{i}", [128, 512]) for i in range(2)]; self.obb = [Buf(), Buf()]
        self.ot = [ar(f"ot{i}", [128, 512]) for i in range(5)]; self.otb = [Buf() for _ in range(5)]
        self.codeT = sb("codeT", [16, 512]); self.codeb = Buf()
        self.ab = sb("ab", [128, 4, 16]); self.abb = Buf()
        self.wout = ar("wout", [128, 8, D], BF16); self.woutb = Buf()
        self.lb = sb("lb", [128, L, 8]); self.oml = sb("oml", [128, L, 8]); self.lbb = Buf()
        lg = sb("lg", [128, L, 8]); sm = sb("lbs", [128, 8])
        P.op("dve", lambda e: e.tensor_copy(out=lg[:], in_=self.PT[:, self.prow["lbl"]:self.prow["lbl"] + 8 * L].rearrange("p (l c) -> p l c", c=8)), reads=[self.smallb], writes=[self.lbb])
        P.op("act", lambda e: e.activation(out=lg[:], in_=lg[:], func=AF.Exp), reads=[self.lbb], writes=[self.lbb])
        P.op("dve", lambda e: e.tensor_copy(out=sm[:], in_=lg[:, 0, :]), reads=[self.lbb], writes=[self.lbb])
        for l in range(1, L):
            P.op("dve", lambda e, l=l: e.tensor_tensor(out=sm[:], in0=sm[:], in1=lg[:, l, :], op=ALU.add), reads=[self.lbb], writes=[self.lbb])
        P.op("dve", lambda e: e.reciprocal(out=sm[:], in_=sm[:]), reads=[self.lbb], writes=[self.lbb])
        P.op("dve", lambda e: e.memset(self.lb[:], 0.0), reads=[self.lbb], writes=[self.lbb])
        for l in range(1, L):
            P.op("dve", lambda e, l=l: e.tensor_tensor(out=lg[:, l, :], in0=lg[:, l, :], in1=sm[:], op=ALU.mult), reads=[self.lbb], writes=[self.lbb])
            P.op("dve", lambda e, l=l: e.tensor_tensor(out=self.lb[:, l, :], in0=self.lb[:, l - 1, :], in1=lg[:, l, :], op=ALU.add), reads=[self.lbb], writes=[self.lbb])
        P.op("dve", lambda e: e.tensor_scalar(out=self.lb[:], in0=self.lb[:], scalar1=0.0, scalar2=1.0, op0=ALU.max, op1=ALU.min), reads=[self.lbb], writes=[self.lbb])
        P.op("dve", lambda e: e.tensor_scalar(out=self.oml[:], in0=self.lb[:], scalar1=-1.0, scalar2=1.0, op0=ALU.mult, op1=ALU.add), reads=[self.lbb], writes=[self.lbb])
        self.wgk_sb = sb("wgk", [16, L, 512]); self.nbgk = sb("nbgk", [128, L, 4]); self.pb_ = Buf()
        P.dma("sp", self.wgk_sb[:], self.wgk.rearrange("l r n -> r l n"), writes=[self.pb_])
        P.op("dve", lambda e: e.tensor_scalar(out=self.nbgk[:], in0=self.PT[:, self.prow["bgk"]:self.prow["bgk"] + 4 * L].rearrange("p (l c) -> p l c", c=4), scalar1=-1.0, scalar2=None, op0=ALU.mult), reads=[self.smallb], writes=[self.pb_])
        self.onA = self.PT[:, self.prow["on_a"]:self.prow["on_a"] + L]
        self.onB = self.PT[:, self.prow["on_b"]:self.prow["on_b"] + 2 * L].rearrange("p (l c) -> p l c", c=2)
        self.onC = self.PT[:, self.prow["on_c"]:self.prow["on_c"] + L]
        self.maskscan = self.cst[:, 1024:1536]

    def load_w(self, l, g):
        if (l, g) in self.wloaded: return self.wloaded[(l, g)]
        P = self.P
        i = self.wrr = (getattr(self, "wrr", -1) + 1) % 2
        n = GN[g]
        P.dma("sp", self.wsb[i][:, 0:8 * n], self.win_img[l, g][:, 0:8 * n], reads=[self.imgdram], writes=[self.wsbb[i]])
        r = (self.wsb[i][:, 0:8 * n].rearrange("p (k n) -> p k n", k=8), self.wsbb[i])
        self.wloaded = {(l, g): r}  if len(self.wloaded) > 1 else {**self.wloaded, (l, g): r}
        return r

    def proj_fm(self, w, wb, off, hT, hb, m=128):
        P = self.P
        pt, ptb = self.ps()
        for k in range(8):
            P.op("pe", lambda e, k=k, pt=pt: e.matmul(pt[0:m, :], lhsT=w[:, k, off:off + m], rhs=hT[:, k, :], start=(k == 0), stop=(k == 7)),
                 reads=[wb, hb], writes=[ptb], inc=(k == 7))
        return pt, ptb

    def proj_tm(self, w, wb, off, n, hT, hb):
        P = self.P
        pt, ptb = self.ps()
        pv = pt[:, 0:4 * n].rearrange("p (t n) -> p t n", t=4)
        for t in range(4):
            for k in range(8):
                P.op("pe", lambda e, k=k, t=t, pv=pv: e.matmul(pv[:, t, :], lhsT=hT[:, k, t * 128:(t + 1) * 128], rhs=w[:, k, off:off + n], start=(k == 0), stop=(k == 7)),
                     reads=[wb, hb], writes=[ptb], inc=(k == 7 and t == 3))
        return pv, ptb

    def emit_out(self, pos, w, wb, goff, moff, hT, hb, onorm, ycs, first, dv):
        P = self.P
        nv = len(pos)
        import os
        VAR = os.environ.get("VAR", "abc")
        for i, (po, pob) in enumerate(pos):
            P.op("dve", lambda e, i=i, po=po: e.tensor_copy(out=self.ob[i][:], in_=po), reads=[pob], writes=[self.obb[i]])
            P.op("pool", lambda e, i=i: e.tensor_tensor(out=self.osq[i][:], in0=self.ob[i][:], in1=self.ob[i][:], op=ALU.mult), reads=[self.obb[i]], writes=[self.osqb[i]])
        pss, pssb = self.ps()
        if "c" not in VAR: return
        for i in range(nv):
            P.op("pe", lambda e, i=i: e.matmul(pss[:], lhsT=self.onesb[:], rhs=self.osq[i][:], start=(i == 0), stop=(i == nv - 1)),
                 reads=[self.osqb[i], self.cstb], writes=[pssb], inc=(i == nv - 1))
        import os
        STOP = int(os.environ.get("STOP", "99"))
        if STOP <= 6: return
        ot, otb = self.ot, self.otb
        P.op("dve", lambda e: e.tensor_scalar(out=ot[0][:], in0=pss[:], scalar1=1.0 / dv, scalar2=EPS, op0=ALU.mult, op1=ALU.add), reads=[pssb], writes=[otb[0]])
        P.op("act", lambda e: e.activation(out=ot[0][:], in_=ot[0][:], func=AF.Sqrt), reads=[otb[0]], writes=[otb[0]])
        P.op("dve", lambda e: e.reciprocal(out=ot[0][:], in_=ot[0][:]), reads=[otb[0]], writes=[otb[0]])
        if STOP <= 7: return
        for i in range(nv):
            pg, pgb = self.proj_fm(w, wb, goff + i * 128, hT, hb)
            pm, pmb = self.proj_fm(w, wb, moff + i * 128, hT, hb)
            P.op("act", lambda e, pg=pg: e.activation(out=ot[1][:], in_=pg[:], func=AF.Silu), reads=[pgb], writes=[otb[1]])
            P.op("act", lambda e, pm=pm: e.activation(out=ot[2][:], in_=pm[:], func=AF.Sigmoid), reads=[pmb], writes=[otb[2]])
            P.op("pool", lambda e: e.tensor_tensor(out=ot[1][:], in0=ot[1][:], in1=ot[2][:], op=ALU.mult), reads=[otb[1], otb[2]], writes=[otb[1]])
            P.op("dve", lambda e, i=i: e.scalar_tensor_tensor(out=ot[3][:], in0=self.ob[i][:], scalar=onorm[i], in1=ot[0][:], op0=ALU.mult, op1=ALU.mult),
                 reads=[self.obb[i], otb[0], self.pb_], writes=[otb[3]])
            yc = ycs[i]
            if STOP <= 8: continue
            if first:
                P.op("dve", lambda e, yc=yc: e.tensor_tensor(out=self.yacc[:, yc, :], in0=ot[3][:], in1=ot[1][:], op=ALU.mult), reads=[otb[3], otb[1]], writes=[self.yb[yc]])
            else:
                P.op("dve", lambda e: e.tensor_tensor(out=ot[3][:], in0=ot[3][:], in1=ot[1][:], op=ALU.mult), reads=[otb[3], otb[1]], writes=[otb[3]])
                P.op("pool", lambda e, yc=yc: e.tensor_tensor(out=self.yacc[:, yc, :], in0=self.yacc[:, yc, :], in1=ot[3][:], op=ALU.add), reads=[otb[3], self.yb[yc]], writes=[self.yb[yc]])

    def gla_head(self, kind, l, b, h, hT, hb):
        P = self.P
        g = h if kind == 'A' else 8 + h
        w, wb = self.load_w(l, g)
        nxt = g + 1
        if nxt < 20: self.load_w(l, nxt)
        tt, ttb = self.tt, self.ttb
        dv = 128 if kind == 'A' else 256
        nv = dv // 128
        if kind == 'A':
            pq, pqb = self.proj_fm(w, wb, 0, hT, hb)
            P.op("act", lambda e: e.activation(out=tt[0][:], in_=pq[:], func=AF.Silu), reads=[pqb], writes=[ttb[0]])
            pf, pfb = self.proj_fm(w, wb, 128, hT, hb)
            P.op("act", lambda e: e.activation(out=tt[1][:], in_=pf[:], func=AF.Sigmoid), reads=[pfb], writes=[ttb[1]])
            P.op("dve", lambda e: e.tensor_scalar(out=tt[1][:], in0=tt[1][:], scalar1=self.oml[:, l, h:h + 1], scalar2=self.lb[:, l, h:h + 1], op0=ALU.mult, op1=ALU.add),
                 reads=[ttb[1], self.lbb], writes=[ttb[1]])
            P.op("act", lambda e: e.activation(out=tt[2][:], in_=tt[1][:], func=AF.Ln), reads=[ttb[1]], writes=[ttb[2]])
            P.op("dve", lambda e: e.tensor_scalar(out=tt[3][:], in0=tt[1][:], scalar1=-1.0, scalar2=1.0, op0=ALU.mult, op1=ALU.add), reads=[ttb[1]], writes=[ttb[3]])
            voff, goff, moff = 256, 384, 512
        else:
            pq, pqb = self.proj_fm(w, wb, 0, hT, hb)
            P.op("act", lambda e: e.activation(out=tt[0][:], in_=pq[:], func=AF.Copy, scale=128.0 ** -0.5), reads=[pqb], writes=[ttb[0]])
            pk, pkb = self.proj_fm(w, wb, 128, hT, hb)
            P.op("act", lambda e: e.activation(out=tt[3][:], in_=pk[:], func=AF.Copy), reads=[pkb], writes=[ttb[3]])
            pg_, pgb_ = self.ps()
            P.op("pe", lambda e: e.matmul(pg_[:], lhsT=self.wgk_sb[:, l, h * 128:(h + 1) * 128], rhs=self.codeT[:], start=True, stop=True), reads=[self.pb_, self.codeb], writes=[pgb_])
            P.op("act", lambda e: e.activation(out=tt[1][:], in_=pg_[:], func=AF.Exp, scale=-1.0, bias=self.nbgk[:, l, h:h + 1]), reads=[pgb_, self.pb_], writes=[ttb[1]])
            P.op("dve", lambda e: e.tensor_scalar(out=tt[1][:], in0=tt[1][:], scalar1=1.0, scalar2=None, op0=ALU.add), reads=[ttb[1]], writes=[ttb[1]])
            P.op("act", lambda e: e.activation(out=tt[2][:], in_=tt[1][:], func=AF.Ln), reads=[ttb[1]], writes=[ttb[2]])
            P.op("dve", lambda e: e.tensor_scalar(out=tt[2][:], in0=tt[2][:], scalar1=-1.0 / 16.0, scalar2=None, op0=ALU.mult), reads=[ttb[2]], writes=[ttb[2]])
            voff, goff, moff = 256, 512, 768
        import os
        STOP = int(os.environ.get("STOP", "99"))
        if STOP <= 1: return
        P.op("dve", lambda e: e.tensor_tensor_scan(out=tt[4][:], data0=self.maskscan, data1=tt[2][:], initial=0.0, op0=ALU.mult, op1=ALU.add), reads=[ttb[2], self.cstb], writes=[ttb[4]])
        b3 = tt[4][:].rearrange("p (c t) -> p c t", t=64)
        d3 = tt[5][:].rearrange("p (c t) -> p c t", t=64)
        P.op("dve", lambda e: e.tensor_tensor(out=d3, in0=b3, in1=b3[:, :, 31:32].broadcast_to([128, 8, 64]), op=ALU.subtract), reads=[ttb[4]], writes=[ttb[5]])
        P.op("act", lambda e: e.activation(out=tt[6][:], in_=tt[5][:], func=AF.Exp), reads=[ttb[5]], writes=[ttb[6]])
        P.op("act", lambda e: e.activation(out=tt[7][:], in_=tt[5][:], func=AF.Exp, scale=-1.0), reads=[ttb[5]], writes=[ttb[7]])
        P.op("act", lambda e: e.activation(out=self.e2[:], in_=b3[:, :, 31:64:32], func=AF.Exp), reads=[ttb[4]], writes=[self.e2b])
        P.op("dve", lambda e: e.tensor_tensor(out=self.qt[:], in0=tt[0][:], in1=tt[6][:], op=ALU.mult), reads=[ttb[0], ttb[6]], writes=[self.qtb])
        P.op("pool", lambda e: e.tensor_tensor(out=self.kt[:], in0=tt[3][:], in1=tt[7][:], op=ALU.mult), reads=[ttb[3], ttb[7]], writes=[self.ktb])
        ep3 = tt[6][:].rearrange("p (c t) -> p c t", t=64)
        P.op("dve", lambda e: e.tensor_tensor(out=self.kh[:].rearrange("p (c t) -> p c t", t=64), in0=self.kt[:].rearrange("p (c t) -> p c t", t=64),
                                               in1=ep3[:, :, 63:64].broadcast_to([128, 8, 64]), op=ALU.mult), reads=[self.ktb, ttb[6]], writes=[self.khb])
        if STOP <= 2: return
        pt, ptb = self.ps()
        pv = pt[:].bitcast(BF16)[:, 0:512].rearrange("p (t n) -> p t n", t=4)
        for t in range(4):
            P.op("pe", lambda e, t=t: e.transpose(out=pv[:, t, :], in_=self.kh[:, t * 128:(t + 1) * 128], identity=self.identb[:]), reads=[self.khb, self.cstb], writes=[ptb], inc=(t == 3))
        P.op("act", lambda e: e.copy(out=self.khT[:], in_=pv), reads=[ptb], writes=[self.khTb])
        for vc in range(nv):
            pvv, pvb = self.proj_tm(w, wb, voff + vc * 128, 128, hT, hb)
            P.op("act", lambda e, vc=vc, pvv=pvv: e.copy(out=self.Vsb[:, :, vc * 128:(vc + 1) * 128], in_=pvv), reads=[pvb], writes=[self.Vb])
        if STOP <= 3: return
        pa, pab = self.ps()
        pav = pa[:].rearrange("p (t n) -> p t n", t=4)
        for t in range(4):
            P.op("pe", lambda e, t=t: e.matmul(pav[:, t, :], lhsT=self.kt[:, t * 128:(t + 1) * 128], rhs=self.qt[:, t * 128:(t + 1) * 128], start=True, stop=True),
                 reads=[self.ktb, self.qtb], writes=[pab], inc=(t == 3))
        P.op("dve", lambda e: e.tensor_tensor(out=self.AT[:], in0=pav, in1=self.cst[:, 128:256].unsqueeze(1).broadcast_to([128, 4, 128]), op=ALU.mult), reads=[pab, self.cstb], writes=[self.ATb])
        if STOP <= 4: return
        if kind == 'A':
            S = self.SA[:, h, :]; Sb = self.SAb[h]
        else:
            S = self.SB[:, h, :]; Sb = self.SBb[h]
        if b == 0:
            P.op("pool", lambda e: e.memset(S, 0.0), writes=[Sb])
        pos = [(self.psum[6 + i], self.pb[6 + i]) for i in range(nv)]
        for c in range(8):
            t, half = c // 2, c % 2
            cs = slice(c * 64, (c + 1) * 64)
            P.op("act", lambda e, c=c: e.activation(out=self.Sp[:, 0:dv], in_=S, func=AF.Copy, scale=self.e2[:, c, 0:1]), reads=[Sb, self.e2b], writes=[self.Spb])
            for vc in range(nv):
                po, pob = pos[vc]
                P.op("pe", lambda e, po=po, vc=vc, cs=cs: e.matmul(po[:, cs], lhsT=self.Sp[:, vc * 128:(vc + 1) * 128], rhs=self.qt[:, cs], start=True, stop=False),
                     reads=[self.Spb, self.qtb], writes=[pob], inc=False)
                P.op("pe", lambda e, po=po, vc=vc, cs=cs, t=t, half=half: e.matmul(po[:, cs], lhsT=self.Vsb[:, t, vc * 128:(vc + 1) * 128], rhs=self.AT[:, t, half * 64:(half + 1) * 64], start=False, stop=True),
                     reads=[self.Vb, self.ATb], writes=[pob], inc=True)
            pk2, pk2b = self.ps()
            hs = slice(half * 64, (half + 1) * 64)
            P.op("pe", lambda e, t=t, hs=hs: e.matmul(pk2[:, 0:dv], lhsT=self.khT[hs, t, :], rhs=self.Vsb[hs, t, 0:dv], start=True, stop=True), reads=[self.khTb, self.Vb], writes=[pk2b])
            P.op("dve", lambda e, c=c: e.scalar_tensor_tensor(out=S, in0=S, scalar=self.e2[:, c, 1:2], in1=pk2[:, 0:dv], op0=ALU.mult, op1=ALU.add), reads=[Sb, self.e2b, pk2b], writes=[Sb])
        if STOP <= 5: return
        onorm = [self.onA[:, l:l + 1]] if kind == 'A' else [self.onB[:, l, 0:1], self.onB[:, l, 1:2]]
        ycs = [h] if kind == 'A' else [2 * h, 2 * h + 1]
        self.emit_out([(po[:], pob) for po, pob in pos], w, wb, goff, moff, hT, hb, onorm, ycs, first=(kind == 'A'), dv=dv)

    def mixer_block(self, l, b, xres, xb, hT, hb, tmp, lnw, lnwb):
        P = self.P
        P.dma("sp", self.wout[:], self.wout_img[l].rearrange("p (k n) -> p k n", k=8), reads=[self.imgdram], writes=[self.woutb])
        self.norm_T(xres, xb, self.lnT["ln_mix_pre"][:, l, :], hT, hb, tmp)
        wm, wmb = self.load_w(l, 20)
        self.wloaded = {}
        pc, pcb = self.proj_fm(wm, wmb, 0, hT, hb, m=16)
        P.op("act", lambda e: e.copy(out=self.codeT[:], in_=pc[0:16, :]), reads=[pcb], writes=[self.codeb])
        pab_, pabb = self.proj_tm(wm, wmb, 16, 16, hT, hb)
        P.op("act", lambda e: e.copy(out=self.ab[:], in_=pab_), reads=[pabb], writes=[self.abb])
        import os
        NA = int(os.environ.get("NA", "8")); NBH = int(os.environ.get("NBH", "4"))
        for h in range(NA): self.gla_head('A', l, b, h, hT, hb)
        for h in range(NBH): self.gla_head('B', l, b, h, hT, hb)
        if self.do_c: self.gdn_block(l, b, hT, hb)
        for c in range(8):
            P.op("act", lambda e, c=c: e.copy(out=self.yT[:, c, :], in_=self.yacc[:, c, :]), reads=[self.yb[c]], writes=[self.yTb])
        for t in range(4):
            pA, pAb = self.ps(); pB, pBb = self.ps()
            for k in range(8):
                for half, (pp, ppb) in enumerate(((pA, pAb), (pB, pBb))):
                    P.op("pe", lambda e, k=k, half=half, pp=pp, t=t: e.matmul(pp[:], lhsT=self.yT[:, k, t * 128:(t + 1) * 128], rhs=self.wout[:, k, half * 512:(half + 1) * 512], start=(k == 0), stop=(k == 7)),
                         reads=[self.yTb, self.woutb], writes=[ppb], inc=(k == 7))
            self.post_norm_add(pA, pAb, pB, pBb, xres, xb, t, lnw[:, 0, :], lnwb, tmp)

    def gdn_block(self, l, b, hT, hb): pass

_mc = make_consts
def make_consts2():
    c = _mc()
    i = np.arange(128)
    c[:, 1536:1664] = (i[:, None] == (64 * (i[None, :] // 64) + 63))
    return c
make_consts = make_consts2

class K4(K3):
    ARENA = 65536
    def c_init(self):
        P = self.P; L = self.DEPTH
        self.cinit = True
        self.SC = self.sb("SC", [128, 8, 128]); self.SCb = [Buf() for _ in range(8)]
        self.ctail = self.sb("ctail", [128, 24, 3]); self.ctb = Buf()
        self.tk = self.sb("tk", [128, 12, 32]); self.tkb = Buf()
        self.hp = self.sb("hp", [128, 2, L * 8]); self.hpb = Buf()
        P.dma("sp", self.hp[:, 0, :], self.alog.rearrange("l h -> (l h)").partition_broadcast(128), writes=[self.hpb], slow=True)
        P.dma("sp", self.hp[:, 1, :], self.dtb.rearrange("l h -> (l h)").partition_broadcast(128), writes=[self.hpb], slow=True)
        P.op("act", lambda e: e.activation(out=self.hp[:, 0, :], in_=self.hp[:, 0, :], func=AF.Exp), reads=[self.hpb], writes=[self.hpb])
        P.op("dve", lambda e: e.tensor_scalar(out=self.hp[:, 0, :], in0=self.hp[:, 0, :], scalar1=-1.0, scalar2=None, op0=ALU.mult), reads=[self.hpb], writes=[self.hpb])
        self.xc = [self.ar([128, 515]) for _ in range(3)]; self.xcb = [Buf() for _ in range(3)]
        self.cb16 = [self.ar([128, 512]) for _ in range(5)]; self.cb16b = [Buf() for _ in range(5)]

    def gdn_block(self, l, b, hT, hb):
        P = self.P
        if not getattr(self, "cinit", False): self.c_init()
        tk, tkb = self.tk, self.tkb
        L = self.DEPTH
        G, BE, LNB, BC, CC, EB, BEB, KD, BL = range(9)
        v3 = lambda i: tk[:, i, :].rearrange("p (t h) -> p t h", h=8)
        a3 = self.ab[:, :, 0:8]; be3 = self.ab[:, :, 8:16]
        bc = lambda ap2: ap2.unsqueeze(1).broadcast_to([128, 4, 8])
        P.op("dve", lambda e: e.tensor_tensor(out=v3(G), in0=a3, in1=bc(self.hp[:, 1, l * 8:(l + 1) * 8]), op=ALU.add), reads=[self.abb, self.hpb], writes=[tkb])
        P.op("act", lambda e: e.activation(out=tk[:, G, :], in_=tk[:, G, :], func=AF.Exp), reads=[tkb], writes=[tkb])
        P.op("dve", lambda e: e.tensor_scalar(out=tk[:, G, :], in0=tk[:, G, :], scalar1=1.0, scalar2=None, op0=ALU.add), reads=[tkb], writes=[tkb])
        P.op("act", lambda e: e.activation(out=tk[:, G, :], in_=tk[:, G, :], func=AF.Ln), reads=[tkb], writes=[tkb])
        P.op("dve", lambda e: e.tensor_tensor(out=v3(G), in0=v3(G), in1=bc(self.hp[:, 0, l * 8:(l + 1) * 8]), op=ALU.mult), reads=[tkb, self.hpb], writes=[tkb])
        P.op("act", lambda e: e.activation(out=v3(BE), in_=be3, func=AF.Sigmoid), reads=[self.abb], writes=[tkb])
        P.op("act", lambda e: e.activation(out=v3(LNB), in_=be3, func=AF.Exp, scale=-1.0), reads=[self.abb], writes=[tkb])
        P.op("dve", lambda e: e.tensor_scalar(out=tk[:, LNB, :], in0=tk[:, LNB, :], scalar1=1.0, scalar2=None, op0=ALU.add), reads=[tkb], writes=[tkb])
        P.op("act", lambda e: e.activation(out=tk[:, LNB, :], in_=tk[:, LNB, :], func=AF.Ln), reads=[tkb], writes=[tkb])
        P.op("dve", lambda e: e.tensor_scalar(out=tk[:, LNB, :], in0=tk[:, LNB, :], scalar1=-1.0, scalar2=None, op0=ALU.mult), reads=[tkb], writes=[tkb])
        pt, ptb = self.ps()
        P.op("pe", lambda e: e.matmul(pt[:, 0:32], lhsT=self.cst[:, 128:256], rhs=tk[:, G, :], start=True, stop=True), reads=[tkb, self.cstb], writes=[ptb])
        P.op("dve", lambda e: e.tensor_copy(out=tk[:, BC, :], in_=pt[:, 0:32]), reads=[ptb], writes=[tkb])
        P.op("dve", lambda e: e.tensor_tensor(out=tk[:, CC, :], in0=tk[:, BC, :], in1=tk[:, LNB, :], op=ALU.add), reads=[tkb], writes=[tkb])
        P.op("act", lambda e: e.activation(out=tk[:, EB, :], in_=tk[:, BC, :], func=AF.Exp), reads=[tkb], writes=[tkb])
        P.op("dve", lambda e: e.tensor_tensor(out=tk[:, BEB, :], in0=tk[:, EB, :], in1=tk[:, BE, :], op=ALU.mult), reads=[tkb], writes=[tkb])
        pt2, pt2b = self.ps()
        P.op("pe", lambda e: e.matmul(pt2[:, 0:32], lhsT=self.cst[:, 1536:1664], rhs=tk[:, BC, :], start=True, stop=True), reads=[tkb, self.cstb], writes=[pt2b])
        P.op("dve", lambda e: e.tensor_tensor(out=tk[:, KD, :], in0=pt2[:, 0:32], in1=tk[:, BC, :], op=ALU.subtract), reads=[pt2b, tkb], writes=[tkb])
        P.op("act", lambda e: e.activation(out=tk[:, KD, :], in_=tk[:, KD, :], func=AF.Exp), reads=[tkb], writes=[tkb])
        if b == 0:
            P.op("pool", lambda e: e.memset(self.ctail[:], 0.0), writes=[self.ctb])
        import os
        for h in range(int(os.environ.get("NC_", "8"))):
            self.gdn_head(l, b, h, hT, hb)

    def gdn_head(self, l, b, h, hT, hb):
        P = self.P
        tk, tkb = self.tk, self.tkb
        G, BE, LNB, BC, CC, EB, BEB, KD, BL = range(9)
        col = lambda i: tk[:, i, :].rearrange("p (t h) -> p t h", h=8)[:, :, h:h + 1]
        w, wb = self.load_w(l, 12 + h)
        if h < 7: self.load_w(l, 13 + h)
        tt, ttb = self.tt, self.ttb
        ot, otb = self.ot, self.otb
        L = self.DEPTH
        for wi in range(3):
            pz, pzb = self.proj_fm(w, wb, wi * 128, hT, hb)
            xc, xcb = self.xc[wi], self.xcb[wi]
            ci = wi * 8 + h
            P.op("pool", lambda e, xc=xc, ci=ci: e.tensor_copy(out=xc[:, 0:3], in_=self.ctail[:, ci, :]), reads=[self.ctb], writes=[xcb])
            P.op("act", lambda e, xc=xc, pz=pz: e.copy(out=xc[:, 3:515], in_=pz[:]), reads=[pzb], writes=[xcb])
            P.op("pool", lambda e, xc=xc, ci=ci: e.tensor_copy(out=self.ctail[:, ci, :], in_=xc[:, 512:515]), reads=[xcb], writes=[self.ctb])
            cw = lambda j, ci=ci: self.PT[:, 128 + l * 96 + j * 24 + ci:128 + l * 96 + j * 24 + ci + 1]
            P.op("dve", lambda e, xc=xc, wi=wi, cw=cw: e.tensor_scalar(out=tt[wi][:], in0=xc[:, 0:512], scalar1=cw(0), scalar2=None, op0=ALU.mult), reads=[xcb, self.smallb], writes=[ttb[wi]])
            for j in range(1, 4):
                P.op("dve", lambda e, xc=xc, wi=wi, cw=cw, j=j: e.scalar_tensor_tensor(out=tt[wi][:], in0=xc[:, j:j + 512], scalar=cw(j), in1=tt[wi][:], op0=ALU.mult, op1=ALU.add),
                     reads=[xcb, self.smallb, ttb[wi]], writes=[ttb[wi]])
            P.op("act", lambda e, wi=wi: e.activation(out=tt[wi][:], in_=tt[wi][:], func=AF.Silu), reads=[ttb[wi]], writes=[ttb[wi]])
        import os
        SC_ = int(os.environ.get('SC_', '99'))
        if SC_ <= 1: return
        for wi in range(2):
            P.op("pool", lambda e, wi=wi: e.tensor_tensor(out=self.osq[wi][:], in0=tt[wi][:], in1=tt[wi][:], op=ALU.mult), reads=[ttb[wi]], writes=[self.osqb[wi]])
            pss, pssb = self.ps()
            P.op("pe", lambda e, wi=wi, pss=pss: e.matmul(pss[:], lhsT=self.onesb[:], rhs=self.osq[wi][:], start=True, stop=True), reads=[self.osqb[wi], self.cstb], writes=[pssb])
            P.op("dve", lambda e, pss=pss: e.tensor_scalar(out=tt[3][:], in0=pss[:], scalar1=EPS, scalar2=None, op0=ALU.add), reads=[pssb], writes=[ttb[3]])
            P.op("act", lambda e: e.activation(out=tt[3][:], in_=tt[3][:], func=AF.Sqrt), reads=[ttb[3]], writes=[ttb[3]])
            P.op("dve", lambda e: e.reciprocal(out=tt[3][:], in_=tt[3][:]), reads=[ttb[3]], writes=[ttb[3]])
            if wi == 0:
                P.op("dve", lambda e: e.scalar_tensor_tensor(out=tt[0][:], in0=tt[0][:], scalar=128.0 ** -0.5, in1=tt[3][:], op0=ALU.mult, op1=ALU.mult), reads=[ttb[0], ttb[3]], writes=[ttb[0]])
            else:
                P.op("dve", lambda e: e.tensor_tensor(out=self.kt[:], in0=tt[1][:], in1=tt[3][:], op=ALU.mult), reads=[ttb[1], ttb[3]], writes=[self.ktb])
        P.op("act", lambda e: e.copy(out=self.kh[:], in_=tt[2][:]), reads=[ttb[2]], writes=[self.khb])
        if h == 0 and b == 0:
            self.dbg("qn", tt[0][:], [ttb[0]]); self.dbg("kn", self.kt[:], [self.ktb]); self.dbg("vv", tt[2][:], [ttb[2]]); self.dbg("tk", tk[:], [tkb])
        for src, srcb, which in ((self.kh, self.khb, 0), (self.kt, self.ktb, 1)):
            pt, ptb = self.ps()
            pv = pt[:].bitcast(BF16)[:, 0:512].rearrange("p (t n) -> p t n", t=4)
            for t in range(4):
                P.op("pe", lambda e, t=t, src=src, pv=pv: e.transpose(out=pv[:, t, :], in_=src[:, t * 128:(t + 1) * 128], identity=self.identb[:]), reads=[srcb, self.cstb], writes=[ptb], inc=(t == 3))
            if which == 0:
                P.op("dve", lambda e, pv=pv: e.tensor_tensor(out=self.Vsb[:, :, 0:128], in0=pv, in1=col(BE).broadcast_to([128, 4, 128]), op=ALU.mult), reads=[ptb, tkb], writes=[self.Vb])
            else:
                P.op("dve", lambda e, pv=pv: e.tensor_tensor(out=self.Vsb[:, :, 128:256], in0=pv, in1=col(BEB).broadcast_to([128, 4, 128]), op=ALU.mult), reads=[ptb, tkb], writes=[self.Vb])
                P.op("dve", lambda e, pv=pv: e.tensor_tensor(out=self.khT[:], in0=pv, in1=col(KD).broadcast_to([128, 4, 128]), op=ALU.mult), reads=[ptb, tkb], writes=[self.khTb])
        if SC_ <= 2: return
        m4 = lambda c0: self.cst[:, c0:c0 + 128].unsqueeze(1).broadcast_to([128, 4, 128])
        t3 = lambda i: tt[i][:].rearrange("p (t n) -> p t n", t=4)
        o3 = lambda i: ot[i][:].rearrange("p (t n) -> p t n", t=4)
        P.op("dve", lambda e: e.tensor_tensor(out=t3(4), in0=m4(128), in1=col(G).broadcast_to([128, 4, 128]), op=ALU.mult), reads=[self.cstb, tkb], writes=[ttb[4]])
        P.op("dve", lambda e: e.tensor_tensor(out=t3(5), in0=m4(0), in1=col(LNB).broadcast_to([128, 4, 128]), op=ALU.mult), reads=[self.cstb, tkb], writes=[ttb[5]])
        P.op("pool", lambda e: e.tensor_tensor(out=tt[5][:], in0=tt[5][:], in1=tt[4][:], op=ALU.add), reads=[ttb[4], ttb[5]], writes=[ttb[5]])
        for i in (4, 5):
            pr, prb = self.ps()
            P.op("pe", lambda e, i=i, pr=pr: e.matmul(pr[:], lhsT=self.ones, rhs=tt[i][:], start=True, stop=True), reads=[ttb[i], self.cstb], writes=[prb])
            P.op("act", lambda e, i=i, pr=pr: e.copy(out=tt[i + 2][:], in_=pr[:]), reads=[prb], writes=[ttb[i + 2]])
        P.op("dve", lambda e: e.tensor_tensor(out=o3(0), in0=t3(6), in1=col(BC).broadcast_to([128, 4, 128]), op=ALU.subtract), reads=[ttb[6], tkb], writes=[otb[0]])
        P.op("pool", lambda e: e.tensor_tensor(out=o3(0), in0=o3(0), in1=m4(512), op=ALU.add), reads=[otb[0], self.cstb], writes=[otb[0]])
        P.op("act", lambda e: e.activation(out=ot[0][:], in_=ot[0][:], func=AF.Exp), reads=[otb[0]], writes=[otb[0]])
        P.op("dve", lambda e: e.tensor_tensor(out=o3(1), in0=t3(7), in1=col(BC).broadcast_to([128, 4, 128]), op=ALU.subtract), reads=[ttb[7], tkb], writes=[otb[1]])
        P.op("pool", lambda e: e.tensor_tensor(out=o3(1), in0=o3(1), in1=m4(640), op=ALU.add), reads=[otb[1], self.cstb], writes=[otb[1]])
        P.op("act", lambda e: e.activation(out=ot[1][:], in_=ot[1][:], func=AF.Exp), reads=[otb[1]], writes=[otb[1]])
        P.op("dve", lambda e: e.tensor_tensor(out=o3(2), in0=t3(6), in1=col(CC).broadcast_to([128, 4, 128]), op=ALU.subtract), reads=[ttb[6], tkb], writes=[otb[2]])
        P.op("pool", lambda e: e.tensor_tensor(out=o3(2), in0=m4(768), in1=o3(2), op=ALU.subtract), reads=[otb[2], self.cstb], writes=[otb[2]])
        P.op("act", lambda e: e.activation(out=ot[2][:], in_=ot[2][:], func=AF.Exp), reads=[otb[2]], writes=[otb[2]])
        if h == 0 and b == 0:
            self.dbg("brow", tt[6][:], [ttb[6]]); self.dbg("crow", tt[7][:], [ttb[7]])
            self.dbg("Einc", ot[0][:], [otb[0]]); self.dbg("EsT", ot[1][:], [otb[1]]); self.dbg("Es", ot[2][:], [otb[2]])
            self.dbg("Vsb", self.Vsb[:], [self.Vb]); self.dbg("khT", self.khT[:], [self.khTb])
        P.op("act", lambda e: e.activation(out=ot[3][:], in_=tt[6][:], func=AF.Exp), reads=[ttb[6]], writes=[otb[3]])
        P.op("dve", lambda e: e.tensor_tensor(out=self.qt[:], in0=tt[0][:], in1=ot[3][:], op=ALU.mult), reads=[ttb[0], otb[3]], writes=[self.qtb])
        P.op("act", lambda e: e.copy(out=self.osq[0][:], in_=tt[0][:]), reads=[ttb[0]], writes=[self.osqb[0]])
        if SC_ <= 3: return
        pkk, pkkb = self.ps(); pkq, pkqb = self.ps()
        kk3 = pkk[:].rearrange("p (t n) -> p t n", t=4); kq3 = pkq[:].rearrange("p (t n) -> p t n", t=4)
        for t in range(4):
            ts_ = slice(t * 128, (t + 1) * 128)
            P.op("pe", lambda e, t=t, ts_=ts_: e.matmul(kk3[:, t, :], lhsT=self.kt[:, ts_], rhs=self.kt[:, ts_], start=True, stop=True), reads=[self.ktb], writes=[pkkb], inc=(t == 3))
        for t in range(4):
            ts_ = slice(t * 128, (t + 1) * 128)
            P.op("pe", lambda e, t=t, ts_=ts_: e.matmul(kq3[:, t, :], lhsT=self.kt[:, ts_], rhs=self.osq[0][:, ts_], start=True, stop=True), reads=[self.ktb, self.osqb[0]], writes=[pkqb], inc=(t == 3))
        cb, cbb = self.cb16, self.cb16b
        P.op("dve", lambda e: e.scalar_tensor_tensor(out=cb[0][:], in0=pkk[:], scalar=-1.0, in1=ot[1][:], op0=ALU.mult, op1=ALU.mult), reads=[pkkb, otb[1]], writes=[cbb[0]])
        P.op("dve", lambda e: e.scalar_tensor_tensor(out=cb[1][:], in0=pkk[:], scalar=-1.0, in1=ot[2][:], op0=ALU.mult, op1=ALU.mult), reads=[pkkb, otb[2]], writes=[cbb[1]])
        P.op("dve", lambda e: e.tensor_tensor(out=self.AT[:].rearrange("p t n -> p (t n)"), in0=pkq[:], in1=ot[0][:], op=ALU.mult), reads=[pkqb, otb[0]], writes=[self.ATb])
        c3 = lambda i: cb[i][:].rearrange("p (t n) -> p t n", t=4)
        Tt, Ttb = self.ob[1], self.obb[1]
        P.op("pool", lambda e: e.tensor_tensor(out=Tt[:].rearrange("p (t n) -> p t n", t=4), in0=c3(0), in1=self.cst[:, 0:128].unsqueeze(1).broadcast_to([128, 4, 128]), op=ALU.add), reads=[cbb[0], self.cstb], writes=[Ttb])
        Pt_i, P_i = 0, 1
        free = [2, 3, 4]
        for r in range(5):
            nP = free.pop(0)
            pp, ppb = self.ps()
            pp3 = pp[:].rearrange("p (t n) -> p t n", t=4)
            for t in range(4):
                P.op("pe", lambda e, t=t, pp3=pp3, a=Pt_i, b_=P_i: e.matmul(pp3[:, t, :], lhsT=c3(a)[:, t, :], rhs=c3(b_)[:, t, :], start=True, stop=True), reads=[cbb[Pt_i], cbb[P_i]], writes=[ppb], inc=(t == 3))
            P.op("act", lambda e, nP=nP, pp=pp: e.copy(out=cb[nP][:], in_=pp[:]), reads=[ppb], writes=[cbb[nP]])
            if r < 4:
                nPt = free.pop(0)
                pq, pqb = self.ps()
                pq3 = pq[:].rearrange("p (t n) -> p t n", t=4)
                for t in range(4):
                    P.op("pe", lambda e, t=t, pq3=pq3, a=P_i, b_=Pt_i: e.matmul(pq3[:, t, :], lhsT=c3(a)[:, t, :], rhs=c3(b_)[:, t, :], start=True, stop=True), reads=[cbb[Pt_i], cbb[P_i]], writes=[pqb], inc=(t == 3))
                P.op("dve", lambda e, nPt=nPt, pq=pq: e.tensor_copy(out=cb[nPt][:], in_=pq[:]), reads=[pqb], writes=[cbb[nPt]])
            pc, pcb = self.ps()
            pc3 = pc[:].rearrange("p (t n) -> p t n", t=4)
            for t in range(4):
                P.op("pe", lambda e, t=t, pc3=pc3, nP=nP: e.matmul(pc3[:, t, :], lhsT=c3(nP)[:, t, :], rhs=Tt[:, t * 128:(t + 1) * 128], start=True, stop=True), reads=[cbb[nP], Ttb], writes=[pcb], inc=(t == 3))
            P.op("dve", lambda e, pc=pc: e.tensor_tensor(out=Tt[:], in0=pc[:], in1=Tt[:], op=ALU.add), reads=[pcb, Ttb], writes=[Ttb])
            free += [Pt_i, P_i]
            if r < 4: Pt_i, P_i = nPt, nP
        if SC_ <= 4: return
        T16, T16b = self.osq[1], self.osqb[1]
        P.op("act", lambda e: e.copy(out=T16[:], in_=Tt[:]), reads=[Ttb], writes=[T16b])
        WT, WTb = self.kh, self.khb
        if h == 0 and b == 0:
            self.dbg("T16", T16[:], [T16b]); self.dbg("qkT", self.AT[:], [self.ATb]); self.dbg("qd", self.qt[:], [self.qtb])
        pw, pwb = self.ps(); pu, pub = self.ps()
        pw3 = pw[:].rearrange("p (t n) -> p t n", t=4); pu3 = pu[:].rearrange("p (t n) -> p t n", t=4)
        for t in range(4):
            P.op("pe", lambda e, t=t: e.matmul(pw3[:, t, :], lhsT=self.Vsb[:, t, 128:256], rhs=T16[:, t * 128:(t + 1) * 128], start=True, stop=True), reads=[self.Vb, T16b], writes=[pwb], inc=(t == 3))
        for t in range(4):
            P.op("pe", lambda e, t=t: e.matmul(pu3[:, t, :], lhsT=T16[:, t * 128:(t + 1) * 128], rhs=self.Vsb[:, t, 0:128], start=True, stop=True), reads=[self.Vb, T16b], writes=[pub], inc=(t == 3))
        P.op("act", lambda e: e.copy(out=WT[:], in_=pw[:]), reads=[pwb], writes=[WTb])
        P.op("dve", lambda e: e.tensor_copy(out=ot[4][:], in_=pu[:]), reads=[pub], writes=[otb[4]])
        if SC_ <= 5: return
        if h == 0 and b == 0:
            self.dbg("U", ot[4][:], [otb[4]]); self.dbg("WT", WT[:], [WTb])
        S = self.SC[:, h, :]; Sb = self.SCb[h]
        if b == 0:
            P.op("pool", lambda e: e.memset(S, 0.0), writes=[Sb])
        Sbf = self.Sp[:, 128:256]; vn = self.Sp[:, 0:128]
        po, pob = self.psum[6], self.pb[6]
        for c in range(8):
            t, half = c // 2, c % 2
            cs = slice(c * 64, (c + 1) * 64); hs = slice(half * 64, (half + 1) * 64)
            P.op("act", lambda e: e.copy(out=Sbf, in_=S), reads=[Sb], writes=[self.Spb])
            pws, pwsb = self.ps()
            P.op("pe", lambda e, t=t, pws=pws: e.matmul(pws[:, 0:128], lhsT=WT[:, t * 128:(t + 1) * 128], rhs=Sbf, start=True, stop=True), reads=[WTb, self.Spb], writes=[pwsb])
            P.op("dve", lambda e, t=t, pws=pws: e.tensor_tensor(out=vn, in0=ot[4][:, t * 128:(t + 1) * 128], in1=pws[:, 0:128], op=ALU.subtract), reads=[otb[4], pwsb, self.Spb], writes=[self.Spb])
            P.op("pe", lambda e, cs=cs: e.matmul(po[:, cs], lhsT=Sbf, rhs=self.qt[:, cs], start=True, stop=False), reads=[self.Spb, self.qtb], writes=[pob], inc=False)
            P.op("pe", lambda e, cs=cs, t=t, half=half: e.matmul(po[:, cs], lhsT=self.Sp[:, 0:128], rhs=self.AT[:, t, half * 64:(half + 1) * 64], start=False, stop=True), reads=[self.Spb, self.ATb], writes=[pob])
            pkv, pkvb = self.ps()
            P.op("pe", lambda e, hs=hs, t=t, pkv=pkv: e.matmul(pkv[:, 0:128], lhsT=self.khT[hs, t, :], rhs=self.Sp[hs, 0:128], start=True, stop=True), reads=[self.khTb, self.Spb], writes=[pkvb])
            al = ot[3][:, c * 64 + 63:c * 64 + 64]
            P.op("dve", lambda e, pkv=pkv, al=al: e.scalar_tensor_tensor(out=S, in0=S, scalar=al, in1=pkv[:, 0:128], op0=ALU.mult, op1=ALU.add), reads=[Sb, otb[3], pkvb], writes=[Sb])
        if h == 0 and b == 0:
            P.op("dve", lambda e: e.tensor_copy(out=self.ob[0][:], in_=po[:]), reads=[pob], writes=[self.obb[0]])
            self.dbg("oT", self.ob[0][:], [self.obb[0]]); self.dbg("Send", S, [Sb])
        self.emit_out([(po[:], pob)], w, wb, 384, 512, hT, hb, [self.onC[:, l:l + 1]], [h], first=False, dv=128)


def kernel(**inputs):
    inputs = {k: np.asarray(v) for k, v in inputs.items()}
    B, T, _ = inputs["x"].shape
    DEPTH = inputs["w_in"].shape[0]
    k = K4(T, DEPTH, do_mix=True, do_mlp=True, do_c=True)
    nc = k.build()
    maps = host_inputs(inputs, T, DEPTH)
    res = run_bass_kernel_spmd(nc, maps, core_ids=list(range(B)))
    return np.stack([np.asarray(r["out"]) for r in res.results]).astype(np.float32)
```
